# Optimizing a Trainium2 kernel written in Bass

```python
import math
import jax, jax.numpy as jnp
from jax import lax
import numpy as np

D_MODEL = 1024
BATCH = 8
SEQ = 4096
DEPTH = 2
DEC_BATCH = 16
DEC_SEQ = 4096
PAST_LEN = 128

ATT_HEADS = 8
ATT_KV_HEADS = 2
ATT_GROUP = ATT_HEADS // ATT_KV_HEADS
ATT_HEAD_DIM = 64
WINDOW = 128
ATT_BLOCK = 128
ROPE_THETA = 500000.0
ROPE_DIM = ATT_HEAD_DIM // 4
HG_HEADS = 8
HG_KEY_DIM = 64
HG_VAL_DIM = 64
HG_CHUNK = 64
D_FF = 4 * D_MODEL
ALPHA = (2 * DEPTH) ** 0.25
BETA = (8 * DEPTH) ** -0.25
LN_EPS = 1e-5
RMS_EPS = 1e-6

ATT_Q = ATT_HEADS * ATT_HEAD_DIM
ATT_KV = ATT_KV_HEADS * ATT_HEAD_DIM
HG_K = HG_HEADS * HG_KEY_DIM
HG_V = HG_HEADS * HG_VAL_DIM
SPLIT_SIZES = (ATT_Q, ATT_KV, ATT_KV, HG_K, HG_K, HG_K, HG_V, HG_V, D_MODEL, D_MODEL)
D_IN = sum(SPLIT_SIZES)

kernel_name = 'hybrid_gated_swa_hgrn2_encoder'


def layer_norm(x, g, b):
    xf = x.astype(jnp.float32)
    mu = jnp.mean(xf, axis=-1, keepdims=True)
    var = jnp.mean(jnp.square(xf - mu), axis=-1, keepdims=True)
    y = (xf - mu) * lax.rsqrt(var + LN_EPS) * g.astype(jnp.float32) + b.astype(jnp.float32)
    return y.astype(x.dtype)


def partial_rope(x, positions):
    inv = ROPE_THETA ** (-jnp.arange(0, ROPE_DIM, 2, dtype=jnp.float32) / ROPE_DIM)
    ang = positions.astype(jnp.float32)[:, None] * inv[None, :]
    cos = jnp.cos(ang)[None, :, None, :]
    sin = jnp.sin(ang)[None, :, None, :]
    xr = x[..., :ROPE_DIM].astype(jnp.float32)
    x1, x2 = xr[..., :ROPE_DIM // 2], xr[..., ROPE_DIM // 2:]
    rot = jnp.concatenate([x1 * cos - x2 * sin, x2 * cos + x1 * sin], axis=-1).astype(x.dtype)
    return jnp.concatenate([rot, x[..., ROPE_DIM:]], axis=-1)


def window_attention(q, k, v, sink):
    B, S = q.shape[0], q.shape[1]
    nb = S // ATT_BLOCK
    qb = q.reshape(B, nb, ATT_BLOCK, ATT_KV_HEADS, ATT_GROUP, ATT_HEAD_DIM)
    pad = ((0, 0), (ATT_BLOCK, ATT_BLOCK), (0, 0), (0, 0))
    kp = jnp.pad(k, pad).reshape(B, nb + 2, ATT_BLOCK, ATT_KV_HEADS, ATT_HEAD_DIM)
    vp = jnp.pad(v, pad).reshape(B, nb + 2, ATT_BLOCK, ATT_KV_HEADS, ATT_HEAD_DIM)
    kw = jnp.concatenate([kp[:, :-2], kp[:, 1:-1], kp[:, 2:]], axis=2)
    vw = jnp.concatenate([vp[:, :-2], vp[:, 1:-1], vp[:, 2:]], axis=2)
    qi = jnp.arange(ATT_BLOCK)[:, None]
    kj = jnp.arange(3 * ATT_BLOCK)[None, :]
    blk = jnp.arange(nb)[:, None, None]
    kpos = blk * ATT_BLOCK + kj - ATT_BLOCK
    mask = (jnp.abs(kj - ATT_BLOCK - qi) <= WINDOW) & (kpos >= 0) & (kpos < S)
    scale = ATT_HEAD_DIM ** -0.5
    s = jnp.einsum('bnqkgd,bnjkd->bnkgqj', qb, kw).astype(jnp.float32) * scale
    s = jnp.where(mask[None, :, None, None, :, :], s, -jnp.inf)
    sk = sink.astype(jnp.float32).reshape(ATT_KV_HEADS, ATT_GROUP)[None, None, :, :, None, None]
    m = jnp.maximum(jnp.max(s, axis=-1, keepdims=True), sk)
    p = jnp.exp(s - m)
    denom = jnp.sum(p, axis=-1, keepdims=True) + jnp.exp(sk - m)
    p = (p / denom).astype(v.dtype)
    o = jnp.einsum('bnkgqj,bnjkd->bnqkgd', p, vw)
    return o.reshape(B, S, ATT_Q)


def gla_chunk_scan(q, k, v, logf):
    B, L, H, dk = q.shape
    dv = v.shape[-1]
    nc = L // HG_CHUNK

    def to_chunks(a):
        return a.reshape(B, nc, HG_CHUNK, H, a.shape[-1]).transpose(1, 0, 3, 2, 4)

    tri = jnp.arange(HG_CHUNK)[:, None] >= jnp.arange(HG_CHUNK)[None, :]

    def step(S, inp):
        qc, kc, vc, gc = inp
        b = jnp.cumsum(gc, axis=2)
        diff = b[:, :, :, None, :] - b[:, :, None, :, :]
        dec = jnp.exp(jnp.where(tri[:, :, None], diff, -jnp.inf))
        A = jnp.einsum('bhtd,bhsd,bhtsd->bhts', qc, kc, dec)
        o = jnp.einsum('bhts,bhse->bhte', A, vc) + jnp.einsum('bhtd,bhde->bhte', qc * jnp.exp(b), S)
        b_last = b[:, :, -1:, :]
        S = jnp.exp(b_last[:, :, 0, :])[..., None] * S + jnp.einsum('bhsd,bhse->bhde', kc * jnp.exp(b_last - b), vc)
        return S, o

    S0 = jnp.zeros((B, H, dk, dv), jnp.float32)
    _, o = lax.scan(step, S0, (to_chunks(q), to_chunks(k), to_chunks(v), to_chunks(logf)))
    return o.transpose(1, 0, 3, 2, 4).reshape(B, L, H, dv)


def hgrn2_branch(hq, hf_fwd, hf_bwd, hi, hg, lower, norm_g):
    B, L = hq.shape[0], hq.shape[1]
    q = jax.nn.silu(hq.astype(jnp.float32)).reshape(B, L, HG_HEADS, HG_KEY_DIM) * HG_KEY_DIM ** -0.5
    v = hi.astype(jnp.float32).reshape(B, L, HG_HEADS, HG_VAL_DIM)

    def gates(f_pre, lb):
        f = lb + (1.0 - lb) * jax.nn.sigmoid(f_pre.astype(jnp.float32))
        f = f.reshape(B, L, HG_HEADS, HG_KEY_DIM)
        return 1.0 - f, jnp.log(f)

    k_f, g_f = gates(hf_fwd, lower[0])
    k_b, g_b = gates(hf_bwd, lower[1])
    o_f = gla_chunk_scan(q, k_f, v, g_f)
    rev = lambda a: jnp.flip(a, axis=1)
    o_b = rev(gla_chunk_scan(rev(q), rev(k_b), rev(v), rev(g_b)))
    o = o_f + o_b
    o = o * lax.rsqrt(jnp.mean(jnp.square(o), axis=-1, keepdims=True) + RMS_EPS) * norm_g.astype(jnp.float32)
    o = o.reshape(B, L, HG_V) * jax.nn.silu(hg.astype(jnp.float32))
    return o.astype(hq.dtype)


def trunk(x, w_in, att_sink, hgrn_lb, hgrn_norm_g, w_proj_att, w_proj_hgrn, w_out,
          ln1_g, ln1_b, w_ff1, w_ff2, ln2_g, ln2_b):
    B, L, _ = x.shape
    positions = jnp.arange(L)
    sm = jax.nn.softmax(hgrn_lb.astype(jnp.float32), axis=0)
    lower = jnp.cumsum(sm, axis=0) - sm[0:1]
    cuts = [int(c) for c in np.cumsum(SPLIT_SIZES)[:-1]]
    for l in range(DEPTH):
        h = x @ w_in[l]
        aq, ak, av, hq, hf_f, hf_b, hi, hg, ga, gb = jnp.split(h, cuts, axis=-1)
        q = partial_rope(aq.reshape(B, L, ATT_HEADS, ATT_HEAD_DIM), positions)
        k = partial_rope(ak.reshape(B, L, ATT_KV_HEADS, ATT_HEAD_DIM), positions)
        v = av.reshape(B, L, ATT_KV_HEADS, ATT_HEAD_DIM)
        o_att = window_attention(q, k, v, att_sink[l])
        o_hg = hgrn2_branch(hq, hf_f, hf_b, hi, hg, lower[l], hgrn_norm_g[l])
        mixed = jax.nn.sigmoid(ga) * (o_att @ w_proj_att[l]) + jax.nn.sigmoid(gb) * (o_hg @ w_proj_hgrn[l])
        x = layer_norm(ALPHA * x + mixed @ w_out[l], ln1_g[l], ln1_b[l])
        ff = jnp.square(jax.nn.relu(x @ w_ff1[l])) @ w_ff2[l]
        x = layer_norm(ALPHA * x + ff, ln2_g[l], ln2_b[l])
    return x


def setup_inputs(seed: int = 0) -> dict:
    key = jax.random.key(seed)
    ks = jax.random.split(key, 16)
    n = jax.random.normal
    f32 = jnp.float32
    return {
        'x_prompt': n(ks[0], (BATCH, SEQ, D_MODEL), f32),
        'x_sample': n(ks[1], (DEC_BATCH, DEC_SEQ, D_MODEL), f32),
        'w_in': n(ks[2], (DEPTH, D_MODEL, D_IN), f32) * D_MODEL ** -0.5,
        'att_sink': n(ks[3], (DEPTH, ATT_HEADS), f32) * 0.5,
        'hgrn_lb': n(ks[4], (DEPTH, 2, HG_K), f32) * 0.5,
        'hgrn_norm_g': 1.0 + 0.02 * n(ks[5], (DEPTH, HG_VAL_DIM), f32),
        'w_proj_att': n(ks[6], (DEPTH, ATT_Q, D_MODEL), f32) * ATT_Q ** -0.5,
        'w_proj_hgrn': n(ks[7], (DEPTH, HG_V, D_MODEL), f32) * HG_V ** -0.5,
        'w_out': n(ks[8], (DEPTH, D_MODEL, D_MODEL), f32) * (D_MODEL ** -0.5 * BETA),
        'ln1_g': 1.0 + 0.02 * n(ks[9], (DEPTH, D_MODEL), f32),
        'ln1_b': 0.02 * n(ks[10], (DEPTH, D_MODEL), f32),
        'w_ff1': n(ks[11], (DEPTH, D_MODEL, D_FF), f32) * D_MODEL ** -0.5,
        'w_ff2': n(ks[12], (DEPTH, D_FF, D_MODEL), f32) * (D_FF ** -0.5 * BETA),
        'ln2_g': 1.0 + 0.02 * n(ks[13], (DEPTH, D_MODEL), f32),
        'ln2_b': 0.02 * n(ks[14], (DEPTH, D_MODEL), f32),
    }


def reference(x_prompt, x_sample, w_in, att_sink, hgrn_lb, hgrn_norm_g, w_proj_att, w_proj_hgrn,
              w_out, ln1_g, ln1_b, w_ff1, w_ff2, ln2_g, ln2_b):
    y_prompt = trunk(x_prompt, w_in, att_sink, hgrn_lb, hgrn_norm_g, w_proj_att, w_proj_hgrn, w_out,
                     ln1_g, ln1_b, w_ff1, w_ff2, ln2_g, ln2_b)
    y_sample = trunk(x_sample, w_in, att_sink, hgrn_lb, hgrn_norm_g, w_proj_att, w_proj_hgrn, w_out,
                     ln1_g, ln1_b, w_ff1, w_ff2, ln2_g, ln2_b)
    return (y_prompt, y_sample)
```

```python
import numpy as np
from contextlib import ExitStack
import concourse.bass as bass
import concourse.mybir as mybir
from concourse.bass_utils import run_bass_kernel_spmd

F32 = mybir.dt.float32
BF16 = mybir.dt.bfloat16
AF = mybir.ActivationFunctionType
ALU = mybir.AluOpType
AX = mybir.AxisListType

D = 1024
DIN = 5376
DFF = 4096
NLAYER = 2
ALPHA = float((2 * NLAYER) ** 0.25)
LN_EPS = 1e-5
RMS_EPS = 1e-6
ROPE_THETA = 500000.0
C_AQ, C_AK, C_AV, C_HQ, C_HFF, C_HFB, C_HI, C_HG, C_GA, C_GB = (
    0, 512, 640, 768, 1280, 1792, 2304, 2816, 3328, 4352)


class Tl:
    __slots__ = ("name", "w", "r")

    def __init__(self, name):
        self.name = name
        self.w = None
        self.r = {}


class Em:
    def __init__(self, nc, st, ndma=56):
        self.nc = nc
        self.eng = {"pe": nc.tensor, "act": nc.scalar, "dve": nc.vector,
                    "pool": nc.gpsimd, "sp": nc.sync}
        self.sem = {e: st.enter_context(nc.semaphore("s_" + e))
                    for e in ("pe", "act", "dve", "pool")}
        self.dpool = {"pool": [st.enter_context(nc.semaphore("dp%d" % i)) for i in range(16)],
                      "sp": [st.enter_context(nc.semaphore("ds%d" % i)) for i in range(40)]}
        self.dcnt = {q: [0] * len(v) for q, v in self.dpool.items()}
        self.ntile = 0
        self.reset_state()

    def reset_state(self):
        self.cnt = {k: 0 for k in self.sem}
        self.dmap = {}
        self.dused = {q: 0 for q in self.dpool}
        self.waited = {}

    def tile(self, name):
        self.ntile += 1
        return Tl("%s#%d" % (name, self.ntile))

    def tiles(self, name, n):
        return [self.tile("%s%d" % (name, i)) for i in range(n)]

    def _semh(self, k):
        return self.sem[k] if isinstance(k, str) else self.dpool[k[0]][k[1]]

    def _wait(self, engname, deps):
        best = {}
        for k, v in deps:
            if v > best.get(k, 0):
                best[k] = v
        for k, v in best.items():
            if engname == "pe" and k == "pe":
                continue
            if self.waited.get((engname, k), 0) >= v:
                continue
            self.eng[engname].wait_ge(self._semh(k), v)
            self.waited[(engname, k)] = v

    @staticmethod
    def _deps(reads, writes):
        deps = []
        for t in reads:
            if t.w:
                deps.append(t.w)
        for t in writes:
            if t.w:
                deps.append(t.w)
            deps.extend(t.r.items())
        return deps

    @staticmethod
    def _record(ev, reads, writes):
        k, v = ev
        for t in reads:
            if t.r.get(k, 0) < v:
                t.r[k] = v
        for t in writes:
            t.w = ev
            t.r = {}

    def op(self, engname, fn, reads=(), writes=()):
        self._wait(engname, self._deps(reads, writes))
        ins = fn(self.eng[engname])
        self.cnt[engname] += 1
        ins.then_inc(self.sem[engname], 1)
        self._record((engname, self.cnt[engname]), reads, writes)

    def dma(self, q, fn, reads=(), writes=(), key=None):
        self._wait(q, self._deps(reads, writes))
        kt = key or (writes[0] if writes else reads[0])
        kn = (kt.name, q)
        if kn not in self.dmap:
            self.dmap[kn] = self.dused[q]
            self.dused[q] += 1
            assert self.dused[q] <= len(self.dpool[q]), "out of dma semaphores on " + q
        i = self.dmap[kn]
        ins = fn(self.eng[q])
        self.dcnt[q][i] += 16
        ins.then_inc(self.dpool[q][i], 16)
        self._record(((q, i), self.dcnt[q][i]), reads, writes)

    def phase_end(self):
        allev = [((q, i), c) for q, v in self.dcnt.items() for i, c in enumerate(v) if c > 0]
        self._wait("sp", allev)
        self.nc.all_engine_barrier()
        for h in self.sem.values():
            self.nc.gpsimd.sem_clear(h)
        self.nc.all_engine_barrier()
        self.reset_state()


class Ring:
    def __init__(self, items):
        self.items = items
        self.i = 0

    def next(self):
        it = self.items[self.i % len(self.items)]
        self.i += 1
        return it


class Cfg:
    def __init__(self, nseq, L, nlayer=NLAYER, upto=99, debug=False):
        self.NSEQ = nseq
        self.L = L
        self.T = nseq * L
        self.NB = self.T // 512
        self.NT = self.T // 128
        self.NC = self.T // 64
        self.nlayer = nlayer
        self.upto = upto
        self.debug = debug


def bc(ap, shape):
    return ap.to_broadcast(list(shape))


def phase_p1(nc, em, cfg, l, dr, cst, xsrc):
    T, NB = cfg.T, cfg.NB
    NTS = cfg.L // 128
    with ExitStack() as st:
        def sb(name, shape, dt):
            return st.enter_context(nc.sbuf_tensor("%s_L%d" % (name, l), shape, dt))

        def pst(name, shape, dt):
            return st.enter_context(nc.psum_tensor("%s_L%d" % (name, l), shape, dt))

        W = sb("p1_w", [128, 8, DIN], BF16)
        tW = em.tiles("W", 8)
        wv = dr["w_in"][l].rearrange("(k p) n -> p k n", p=128)
        for k in range(8):
            em.dma("pool", lambda e, k=k: e.dma_start(out=W[:, k, :], in_=wv[:, k, :]),
                   writes=[tW[k]])

        xb = [sb("p1_xb%d" % i, [128, D], BF16) for i in range(8)]
        t_xb = em.tiles("xb", 8)
        rX = Ring(list(zip(xb, t_xb)))
        xq = []

        def load_x(n):
            xbi, txb = rX.next()
            em.dma("pool", lambda e: e.dma_start(out=xbi[:], in_=xsrc[n * 128:(n + 1) * 128, :]), writes=[txb])
            xq.append((xbi, txb))
        for n in range(4):
            load_x(n)
        xT = [sb("p1_xT%d" % i, [128, 8, 512], BF16) for i in range(2)]
        t_xT = [em.tiles("xT%d_" % i, 4) for i in range(2)]
        pT = [pst("p1_pT%d" % i, [128, 1024], BF16) for i in range(2)]
        t_pT = em.tiles("pT", 2)
        pA = [pst("p1_pA%d" % i, [128, 512], F32) for i in range(2)]
        t_pA = em.tiles("pA", 2)
        pF = [pst("p1_pF%d" % i, [128, 512], F32) for i in range(3)]
        t_pF = em.tiles("pF", 3)
        pQ = pst("p1_pQ", [128, 1024], BF16)
        t_pQ = em.tile("pQ")
        rT, rA, rF = Ring(list(zip(pT, t_pT))), Ring(list(zip(pA, t_pA))), Ring(list(zip(pF, t_pF)))

        qrot = [sb("p1_qrot%d" % i, [128, 8, 64], BF16) for i in range(2)]
        t_qrot = em.tiles("qrot", 2)
        krot = [sb("p1_krot%d" % i, [128, 2, 64], BF16) for i in range(2)]
        t_krot = em.tiles("krot", 2)
        vst = [sb("p1_v%d" % i, [128, 128], BF16) for i in range(2)]
        t_vst = em.tiles("vst", 2)
        rtmp = [sb("p1_rtmp%d" % i, [128, 4, 10, 8], F32) for i in range(2)]
        t_rtmp = [em.tiles("rtmp%d_" % i, 8) for i in range(2)]
        qts = [sb("p1_qts%d" % i, [64, 10, 128], BF16) for i in range(2)]
        t_qts = em.tiles("qts", 2)
        his = [sb("p1_his%d" % i, [128, 512], BF16) for i in range(2)]
        t_his = em.tiles("his", 2)
        hgs = [sb("p1_hgs%d" % i, [128, 512], BF16) for i in range(2)]
        t_hgs = em.tiles("hgs", 2)
        sq = sb("p1_sq", [128, 4, 512], F32)
        t_sq = em.tiles("sq", 4)
        NG = 2
        sig = [sb("p1_sig%d" % i, [128, 512], F32) for i in range(3)]
        t_sig = em.tiles("sig", 3)
        gg = [sb("p1_g%d" % i, [128, 512], F32) for i in range(NG)]
        t_g = em.tiles("g", NG)
        bb = [sb("p1_b%d" % i, [128, 512], F32) for i in range(NG)]
        t_b = em.tiles("b", NG)
        rel = [sb("p1_rel%d" % i, [128, 512], F32) for i in range(NG)]
        t_rel = em.tiles("rel", NG)
        eq = [sb("p1_eq%d" % i, [128, 512], F32) for i in range(NG)]
        t_eq = em.tiles("eq", NG)
        ek = [sb("p1_ek%d" % i, [128, 512], F32) for i in range(NG)]
        t_ek = em.tiles("ek", NG)
        tmr = [sb("p1_tmr%d" % i, [128, 8], F32) for i in range(NG)]
        t_tmr = em.tiles("tmr", NG)
        qk = [sb("p1_qk%d" % i, [128, 2, 4, 512], BF16) for i in range(2)]
        t_qk = [[[em.tile("qk") for _ in range(4)] for _ in range(2)] for _ in range(2)]
        scs = [sb("p1_scs%d" % i, [128, 6, 4, 8], F32) for i in range(2)]
        t_scs = [em.tiles("scs%d_" % i, 8) for i in range(2)]
        sg = [sb("p1_sg%d" % i, [128, 8, 512], BF16) for i in range(2)]
        t_sg = [em.tiles("sg%d_" % i, 8) for i in range(2)]

        ident = cst["ident"]
        lb_i = lambda d, p: cst["lbt"][:, (l * 2 + d) * 4 + p:(l * 2 + d) * 4 + p + 1]
        oml_i = lambda d, p: cst["oml"][:, (l * 2 + d) * 4 + p:(l * 2 + d) * 4 + p + 1]

        gidx = 0
        for b in range(NB):
            tau0 = b * 512
            xTi, txT = xT[b % 2], t_xT[b % 2]
            tqk = t_qk
            scsi, tscs = scs[b % 2], t_scs[b % 2]
            def emit_xT(nb):
                xTn, txTn = xT[nb % 2], t_xT[nb % 2]
                for tt in range(4):
                    p_, tp_ = rT.next()
                    xbi, txb = xq.pop(0)

                    def f(pe, p_=p_, xbi=xbi):
                        for k in range(8):
                            ins = pe.transpose(out=p_[:, k * 128:(k + 1) * 128],
                                               in_=xbi[:, k * 128:(k + 1) * 128], identity=ident[:])
                        return ins
                    em.op("pe", f, reads=[txb], writes=[tp_])
                    em.op("act", lambda e, p_=p_, tt=tt: e.activation(
                        out=xTn[:, :, tt * 128:(tt + 1) * 128],
                        in_=p_[:].rearrange("p (k j) -> p k j", j=128), func=AF.Copy),
                        reads=[tp_], writes=[txTn[tt]])
            if b == 0:
                emit_xT(0)
            for tt in range(4):
                ti = (tau0 // 128 + tt)
                tis = ti % NTS
                cosb = cst["cos"][:, tis, :]
                sinb = cst["sin"][:, tis, :]
                s2 = ti % 2
                p_, tp_ = rA.next()

                def fq(pe, p_=p_, tt=tt):
                    for k in range(8):
                        ins = pe.matmul(p_[:], lhsT=xTi[:, k, tt * 128:(tt + 1) * 128],
                                        rhs=W[:, k, C_AQ:C_AQ + 512], start=(k == 0), stop=(k == 7))
                    return ins
                em.op("pe", fq, reads=[txT[tt]] + tW, writes=[tp_])
                pv = p_[:].rearrange("p (h e) -> p h e", e=64)
                rt, trt = rtmp[s2], t_rtmp[s2]
                qr, tqr = qrot[s2], t_qrot[s2]
                cb8 = bc(cosb.unsqueeze(1), [128, 8, 8])
                sb8 = bc(sinb.unsqueeze(1), [128, 8, 8])
                em.op("dve", lambda e, pv=pv, rt=rt, cb8=cb8: e.tensor_tensor(
                    out=rt[:, 0, 0:8, :], in0=pv[:, :, 0:8], in1=cb8, op=ALU.mult), reads=[tp_], writes=[trt[0]])
                em.op("dve", lambda e, pv=pv, rt=rt, sb8=sb8: e.tensor_tensor(
                    out=rt[:, 1, 0:8, :], in0=pv[:, :, 8:16], in1=sb8, op=ALU.mult), reads=[tp_], writes=[trt[1]])
                em.op("dve", lambda e, pv=pv, rt=rt, cb8=cb8: e.tensor_tensor(
                    out=rt[:, 2, 0:8, :], in0=pv[:, :, 8:16], in1=cb8, op=ALU.mult), reads=[tp_], writes=[trt[2]])
                em.op("dve", lambda e, pv=pv, rt=rt, sb8=sb8: e.tensor_tensor(
                    out=rt[:, 3, 0:8, :], in0=pv[:, :, 0:8], in1=sb8, op=ALU.mult), reads=[tp_], writes=[trt[3]])
                em.op("pool", lambda e, rt=rt, qr=qr: e.tensor_tensor(
                    out=qr[:, :, 0:8], in0=rt[:, 0, 0:8, :], in1=rt[:, 1, 0:8, :], op=ALU.subtract),
                    reads=trt[0:2], writes=[tqr])
                em.op("pool", lambda e, rt=rt, qr=qr: e.tensor_tensor(
                    out=qr[:, :, 8:16], in0=rt[:, 2, 0:8, :], in1=rt[:, 3, 0:8, :], op=ALU.add),
                    reads=trt[2:4], writes=[tqr])
                em.op("act", lambda e, pv=pv, qr=qr: e.activation(
                    out=qr[:, :, 16:64], in_=pv[:, :, 16:64], func=AF.Copy), reads=[tp_], writes=[tqr])
                p2, tp2 = rA.next()

                def fkv(pe, p2=p2, tt=tt):
                    for k in range(8):
                        ins = pe.matmul(p2[:, 0:256], lhsT=xTi[:, k, tt * 128:(tt + 1) * 128],
                                        rhs=W[:, k, C_AK:C_AK + 256], start=(k == 0), stop=(k == 7))
                    return ins
                em.op("pe", fkv, reads=[txT[tt]] + tW, writes=[tp2])
                kv_ = p2[:, 0:128].rearrange("p (h e) -> p h e", e=64)
                kr, tkr = krot[s2], t_krot[s2]
                cb2 = bc(cosb.unsqueeze(1), [128, 2, 8])
                sb2 = bc(sinb.unsqueeze(1), [128, 2, 8])
                em.op("dve", lambda e, kv_=kv_, rt=rt, cb2=cb2: e.tensor_tensor(
                    out=rt[:, 0, 8:10, :], in0=kv_[:, :, 0:8], in1=cb2, op=ALU.mult), reads=[tp2], writes=[trt[4]])
                em.op("dve", lambda e, kv_=kv_, rt=rt, sb2=sb2: e.tensor_tensor(
                    out=rt[:, 1, 8:10, :], in0=kv_[:, :, 8:16], in1=sb2, op=ALU.mult), reads=[tp2], writes=[trt[5]])
                em.op("dve", lambda e, kv_=kv_, rt=rt, cb2=cb2: e.tensor_tensor(
                    out=rt[:, 2, 8:10, :], in0=kv_[:, :, 8:16], in1=cb2, op=ALU.mult), reads=[tp2], writes=[trt[6]])
                em.op("dve", lambda e, kv_=kv_, rt=rt, sb2=sb2: e.tensor_tensor(
                    out=rt[:, 3, 8:10, :], in0=kv_[:, :, 0:8], in1=sb2, op=ALU.mult), reads=[tp2], writes=[trt[7]])
                em.op("pool", lambda e, rt=rt, kr=kr: e.tensor_tensor(
                    out=kr[:, :, 0:8], in0=rt[:, 0, 8:10, :], in1=rt[:, 1, 8:10, :], op=ALU.subtract),
                    reads=trt[4:6], writes=[tkr])
                em.op("pool", lambda e, rt=rt, kr=kr: e.tensor_tensor(
                    out=kr[:, :, 8:16], in0=rt[:, 2, 8:10, :], in1=rt[:, 3, 8:10, :], op=ALU.add),
                    reads=trt[6:8], writes=[tkr])
                em.op("act", lambda e, kv_=kv_, kr=kr: e.activation(
                    out=kr[:, :, 16:64], in_=kv_[:, :, 16:64], func=AF.Copy), reads=[tp2], writes=[tkr])
                vs, tvs = vst[s2], t_vst[s2]
                em.op("act", lambda e, p2=p2, vs=vs: e.activation(
                    out=vs[:], in_=p2[:, 128:256], func=AF.Copy), reads=[tp2], writes=[tvs])
                em.dma("sp", lambda e, vs=vs, ti=ti: e.dma_start(
                    out=dr["V"][ti * 128:(ti + 1) * 128, :], in_=vs[:]), reads=[tvs])
                p3, tp3 = rA.next()

                def fhi(pe, p3=p3, tt=tt):
                    for k in range(8):
                        ins = pe.matmul(p3[:], lhsT=xTi[:, k, tt * 128:(tt + 1) * 128],
                                        rhs=W[:, k, C_HI:C_HI + 512], start=(k == 0), stop=(k == 7))
                    return ins
                em.op("pe", fhi, reads=[txT[tt]] + tW, writes=[tp3])
                hs, ths = his[s2], t_his[s2]
                em.op("act", lambda e, p3=p3, hs=hs: e.activation(out=hs[:], in_=p3[:], func=AF.Copy),
                      reads=[tp3], writes=[ths])
                em.dma("sp", lambda e, hs=hs, ti=ti: e.dma_start(
                    out=dr["HI"][ti * 128:(ti + 1) * 128, :], in_=hs[:]), reads=[ths])
                p4, tp4 = rA.next()

                def fhg(pe, p4=p4, tt=tt):
                    for k in range(8):
                        ins = pe.matmul(p4[:], lhsT=xTi[:, k, tt * 128:(tt + 1) * 128],
                                        rhs=W[:, k, C_HG:C_HG + 512], start=(k == 0), stop=(k == 7))
                    return ins
                em.op("pe", fhg, reads=[txT[tt]] + tW, writes=[tp4])
                gs, tgs = hgs[s2], t_hgs[s2]
                em.op("act", lambda e, p4=p4, gs=gs: e.activation(out=gs[:], in_=p4[:], func=AF.Silu),
                      reads=[tp4], writes=[tgs])
                em.dma("sp", lambda e, gs=gs, ti=ti: e.dma_start(
                    out=dr["SHG"][ti * 128:(ti + 1) * 128, :], in_=gs[:]), reads=[tgs])

                pk_, tpk_ = rT.next()

                def ftr(pe, qr=qr, kr=kr, pk_=pk_):
                    for h in range(8):
                        pe.transpose(out=pQ[0:64, h * 128:(h + 1) * 128], in_=qr[:, h, :], identity=ident[:])
                    for h in range(2):
                        ins = pe.transpose(out=pk_[0:64, h * 128:(h + 1) * 128], in_=kr[:, h, :], identity=ident[:])
                    return ins
                em.op("pe", ftr, reads=[tqr, tkr], writes=[t_pQ, tpk_])
                qt_, tqt = qts[s2], t_qts[s2]
                em.op("dve", lambda e, qt_=qt_: e.tensor_copy(
                    out=qt_[:, 0:8, :], in_=pQ[0:64, :].rearrange("p (h j) -> p h j", j=128)),
                    reads=[t_pQ], writes=[tqt])
                em.op("dve", lambda e, qt_=qt_, pk_=pk_: e.tensor_copy(
                    out=qt_[:, 8:10, :], in_=pk_[0:64, 0:256].rearrange("p (h j) -> p h j", j=128)),
                    reads=[tpk_], writes=[tqt])
                em.dma("sp", lambda e, qt_=qt_, ti=ti: e.dma_start(
                    out=dr["QT"][:, :, ti * 128:(ti + 1) * 128], in_=qt_[:, 0:8, :]), reads=[tqt])
                em.dma("sp", lambda e, qt_=qt_, ti=ti: e.dma_start(
                    out=dr["KT"][:, :, ti * 128:(ti + 1) * 128], in_=qt_[:, 8:10, :]), reads=[tqt])

            if b + 1 < NB:
                for n in range(4):
                    load_x((b + 1) * 4 + n)

            def fmm(col0):
                p_, tp_ = rF.next()

                def f(pe, p_=p_):
                    for k in range(8):
                        ins = pe.matmul(p_[:], lhsT=W[:, k, col0:col0 + 128], rhs=xTi[:, k, :],
                                        start=(k == 0), stop=(k == 7))
                    return ins
                em.op("pe", f, reads=txT + tW, writes=[tp_])
                return p_, tp_

            for p in range(4):
                p_, tp_ = fmm(C_HQ + p * 128)
                em.op("act", lambda e, p_=p_, p=p: e.activation(out=sq[:, p, :], in_=p_[:], func=AF.Silu),
                      reads=[tp_], writes=[t_sq[p]])
            gates = [(gsel, ft) for gsel in range(2) for ft in range(8)]

            def gate_group(gsel, ft):
                p_, tp_ = fmm((C_GA if gsel == 0 else C_GB) + ft * 128)
                em.op("act", lambda e: e.activation(
                    out=sg[gsel][:, ft, :], in_=p_[:], func=AF.Sigmoid), reads=[tp_], writes=[t_sg[gsel][ft]])
                if ft == 7:
                    dst = dr["SGA" if gsel == 0 else "SGB"]
                    em.dma("sp", lambda e: e.dma_start(
                        out=dst[:, tau0:tau0 + 512].rearrange("(f q) t -> q f t", q=128),
                        in_=sg[gsel][:]), reads=t_sg[gsel])

            def stage_a(n):
                d, p = divmod(n, 4)
                i3 = n % 3
                p_, tp_ = fmm((C_HFF if d == 0 else C_HFB) + p * 128)
                em.op("act", lambda e: e.activation(out=sig[i3][:], in_=p_[:], func=AF.Sigmoid),
                      reads=[tp_], writes=[t_sig[i3]])
                em.op("dve", lambda e: e.tensor_scalar(
                    out=sig[i3][:], in0=sig[i3][:], scalar1=oml_i(d, p), scalar2=lb_i(d, p),
                    op0=ALU.mult, op1=ALU.add), reads=[t_sig[i3]], writes=[t_sig[i3]])

            def stage_b(n):
                d, p = divmod(n, 4)
                i3, gi = n % 3, n % 2
                em.op("act", lambda e: e.activation(out=gg[gi][:], in_=sig[i3][:], func=AF.Ln),
                      reads=[t_sig[i3]], writes=[t_g[gi]])
                em.op("pool", lambda e: e.tensor_scalar(
                    out=sig[i3][:], in0=sig[i3][:], scalar1=-1.0, scalar2=1.0, op0=ALU.mult, op1=ALU.add),
                    reads=[t_sig[i3]], writes=[t_sig[i3]])
                em.op("dve", lambda e: e.tensor_tensor_scan(
                    out=bb[gi][:], data0=cst["cmask"][:], data1=gg[gi][:], initial=0.0,
                    op0=ALU.mult, op1=ALU.add), reads=[t_g[gi]], writes=[t_b[gi]])
                bv = bb[gi][:].rearrange("p (c j) -> p c j", j=64)
                rv = rel[gi][:].rearrange("p (c j) -> p c j", j=64)
                gv = gg[gi][:].rearrange("p (c j) -> p c j", j=64)
                if d == 0:
                    em.op("pool", lambda e: e.tensor_tensor(
                        out=rv, in0=bv, in1=bc(bv[:, :, 31:32], [128, 8, 64]), op=ALU.subtract),
                        reads=[t_b[gi]], writes=[t_rel[gi]])
                else:
                    em.op("pool", lambda e: e.tensor_tensor(
                        out=gg[gi][:], in0=bb[gi][:], in1=gg[gi][:], op=ALU.subtract),
                        reads=[t_b[gi], t_g[gi]], writes=[t_g[gi]])
                    em.op("pool", lambda e: e.tensor_tensor(
                        out=rv, in0=gv, in1=bc(gv[:, :, 32:33], [128, 8, 64]), op=ALU.subtract),
                        reads=[t_g[gi]], writes=[t_rel[gi]])
                    em.op("pool", lambda e: e.tensor_tensor(
                        out=tmr[gi][:], in0=bv[:, :, 63], in1=gv[:, :, 32], op=ALU.subtract),
                        reads=[t_b[gi], t_g[gi]], writes=[t_tmr[gi]])

            def stage_c(n):
                d, p = divmod(n, 4)
                i3, gi = n % 3, n % 2
                bv = bb[gi][:].rearrange("p (c j) -> p c j", j=64)
                rv = rel[gi][:].rearrange("p (c j) -> p c j", j=64)
                gv = gg[gi][:].rearrange("p (c j) -> p c j", j=64)
                sgn = 1.0 if d == 0 else -1.0
                em.op("act", lambda e: e.activation(
                    out=eq[gi][:], in_=rel[gi][:], func=AF.Exp, scale=sgn), reads=[t_rel[gi]], writes=[t_eq[gi]])
                em.op("act", lambda e: e.activation(
                    out=ek[gi][:], in_=rel[gi][:], func=AF.Exp, scale=-sgn), reads=[t_rel[gi]], writes=[t_ek[gi]])
                tsc = tscs[d * 4 + p]
                em.op("act", lambda e: e.activation(
                    out=scsi[:, 3 * d + 0, p, :], in_=bv[:, :, 63], func=AF.Exp), reads=[t_b[gi]], writes=[tsc])
                if d == 0:
                    em.op("act", lambda e: e.activation(
                        out=scsi[:, 1, p, :], in_=rv[:, :, 63], func=AF.Exp), reads=[t_rel[gi]], writes=[tsc])
                    em.op("act", lambda e: e.activation(
                        out=scsi[:, 2, p, :], in_=bv[:, :, 31], func=AF.Exp), reads=[t_b[gi]], writes=[tsc])
                else:
                    em.op("act", lambda e: e.activation(
                        out=scsi[:, 4, p, :], in_=gv[:, :, 32], func=AF.Exp), reads=[t_g[gi]], writes=[tsc])
                    em.op("act", lambda e: e.activation(
                        out=scsi[:, 5, p, :], in_=tmr[gi][:], func=AF.Exp), reads=[t_tmr[gi]], writes=[tsc])
                em.op("dve", lambda e: e.scalar_tensor_tensor(
                    out=qk[d][:, 0, p, :], in0=sq[:, p, :], scalar=0.125, in1=eq[gi][:],
                    op0=ALU.mult, op1=ALU.mult), reads=[t_sq[p], t_eq[gi]], writes=[tqk[d][0][p]])
                em.op("pool", lambda e: e.tensor_tensor(
                    out=qk[d][:, 1, p, :], in0=sig[i3][:], in1=ek[gi][:], op=ALU.mult),
                    reads=[t_sig[i3], t_ek[gi]], writes=[tqk[d][1][p]])
                if p == 3:
                    for w_, nm in ((0, "QM"), (1, "KM")):
                        dst = dr[nm + ("F" if d == 0 else "B")]
                        em.dma("sp", lambda e, dst=dst, w_=w_: e.dma_start(
                            out=dst[:, tau0:tau0 + 512].rearrange("(p q) t -> q p t", q=128),
                            in_=qk[d][:, w_, :, :]), reads=tqk[d][w_], key=tqk[d][w_][0])

            gq = list(gates)
            for it in range(10):
                if it < 8:
                    stage_a(it)
                    for _ in range(2):
                        gate_group(*gq.pop(0))
                if it == 8 and b + 1 < NB:
                    emit_xT(b + 1)
                if 1 <= it <= 8:
                    stage_b(it - 1)
                if it >= 2:
                    stage_c(it - 2)
            em.dma("sp", lambda e: e.dma_start(out=dr["SC"][b], in_=scsi[:]), reads=tscs)
        em.phase_end()


def phase_p2(nc, em, cfg, l, dr, cst):
    L, NSEQ = cfg.L, cfg.NSEQ
    NTS = L // 128
    with ExitStack() as st:
        def sb(name, shape, dt):
            return st.enter_context(nc.sbuf_tensor("%s_L%d" % (name, l), shape, dt))

        def pst(name, shape, dt):
            return st.enter_context(nc.psum_tensor("%s_L%d" % (name, l), shape, dt))
        KTs = [sb("p2_kt%d" % i, [64, 2, L], BF16) for i in range(2)]
        t_KT = em.tiles("p2kt", 2)
        Vs = [sb("p2_v%d" % i, [128, NTS, 128], BF16) for i in range(2)]
        t_V = em.tiles("p2v", 2)
        rQ = Ring([(sb("p2_q%d" % i, [64, 8, 128], BF16), em.tile("p2q")) for i in range(3)])
        rE = Ring([(sb("p2_e%d" % i, [128, 512], BF16), em.tile("p2e")) for i in range(8)])
        rS = Ring([(pst("p2_ps%d" % i, [128, 512], F32), em.tile("p2ps")) for i in range(3)])
        rO = Ring([(pst("p2_po%d" % i, [64, 512], F32), em.tile("p2po")) for i in range(2)])
        rD = Ring([(pst("p2_pd%d" % i, [64, 512], F32), em.tile("p2pd")) for i in range(2)])
        rDs = Ring([(sb("p2_ds%d" % i, [64, 512], F32), em.tile("p2ds")) for i in range(2)])
        rOa = Ring([(sb("p2_oa%d" % i, [64, 512], BF16), em.tile("p2oa")) for i in range(3)])
        ones = cst["ones"]
        def load_seq(s):
            kt, tkt = KTs[s % 2], t_KT[s % 2]
            vv, tv = Vs[s % 2], t_V[s % 2]
            em.dma("sp", lambda e: e.dma_start(out=kt[:], in_=dr["KT"][:, :, s * L:(s + 1) * L]), writes=[tkt])
            em.dma("sp", lambda e: e.dma_start(
                out=vv[:], in_=dr["V"][s * L:(s + 1) * L, :].rearrange("(i p) f -> p i f", p=128)), writes=[tv])

        def load_q(n):
            s, i = divmod(n, NTS)
            tau = s * L + i * 128
            qt, tq = rQ.next()
            em.dma("sp", lambda e: e.dma_start(out=qt[:], in_=dr["QT"][:, :, tau:tau + 128]), writes=[tq])
            return qt, tq

        def emit_s(s, i, g, qt, tq):
            kt, tkt = KTs[s % 2], t_KT[s % 2]
            js = [j for j in (i - 1, i, i + 1) if 0 <= j < NTS]
            es = []
            for j in js:
                ps_, tps = rS.next()
                em.op("pe", lambda pe: pe.matmul(
                    ps_[:], lhsT=kt[:, g, j * 128:(j + 1) * 128],
                    rhs=qt[:, 4 * g:4 * g + 4, :].rearrange("p h q -> p (h q)"), start=True, stop=True),
                    reads=[tkt, tq], writes=[tps])
                e_, te = rE.next()
                em.op("act", lambda e: e.activation(out=e_[:], in_=ps_[:], func=AF.Exp, scale=0.125),
                      reads=[tps], writes=[te])
                if j != i:
                    m = 0 if j < i else 1
                    ev = e_[:].rearrange("p (h q) -> p h q", q=128)
                    em.op("pool", lambda e: e.tensor_tensor(
                        out=ev, in0=ev, in1=bc(cst["band"][:, m, :].unsqueeze(1), [128, 4, 128]),
                        op=ALU.mult), reads=[te], writes=[te])
                es.append((j, e_, te))
            return es

        def emit_pv(s, i, g, es):
            vv, tv = Vs[s % 2], t_V[s % 2]
            tau = s * L + i * 128
            po, tpo = rO.next()
            pd, tpd = rD.next()

            def fo(pe):
                for n, (j, e_, te) in enumerate(es):
                    ins = pe.matmul(po[:], lhsT=vv[:, j, g * 64:(g + 1) * 64], rhs=e_[:],
                                    start=(n == 0), stop=(n == len(es) - 1))
                return ins
            em.op("pe", fo, reads=[tv] + [x[2] for x in es], writes=[tpo])

            def fd(pe):
                for n, (j, e_, te) in enumerate(es):
                    ins = pe.matmul(pd[:], lhsT=ones[:, 0:64], rhs=e_[:],
                                    start=(n == 0), stop=(n == len(es) - 1))
                return ins
            em.op("pe", fd, reads=[x[2] for x in es], writes=[tpd])
            ds, tds = rDs.next()
            esk = bc(cst["esink"][:, l * 8 + 4 * g:l * 8 + 4 * g + 4].unsqueeze(2), [64, 4, 128])
            em.op("dve", lambda e: e.tensor_tensor(
                out=ds[:].rearrange("p (h q) -> p h q", q=128),
                in0=pd[:].rearrange("p (h q) -> p h q", q=128), in1=esk, op=ALU.add),
                reads=[tpd], writes=[tds])
            em.op("act", lambda e: e.activation(out=ds[:], in_=ds[:], func=AF.Ln), reads=[tds], writes=[tds])
            em.op("act", lambda e: e.activation(out=ds[:], in_=ds[:], func=AF.Exp, scale=-1.0), reads=[tds], writes=[tds])
            oa, toa = rOa.next()
            em.op("dve", lambda e: e.tensor_tensor(out=oa[:], in0=po[:], in1=ds[:], op=ALU.mult),
                  reads=[tpo, tds], writes=[toa])
            em.dma("sp", lambda e: e.dma_start(
                out=dr["OATT"].rearrange("(h d) t -> d h t", d=64)[:, 4 * g:4 * g + 4, tau:tau + 128],
                in_=oa[:].rearrange("p (h q) -> p h q", q=128)), reads=[toa])

        ntile = NSEQ * NTS
        load_seq(0)
        qnext = load_q(0)
        prev = None
        for n in range(ntile):
            s, i = divmod(n, NTS)
            qt, tq = qnext
            if i == min(1, NTS - 1) and s + 1 < NSEQ:
                load_seq(s + 1)
            if n + 1 < ntile:
                qnext = load_q(n + 1)
            for g in range(2):
                es = emit_s(s, i, g, qt, tq)
                if prev is not None:
                    emit_pv(*prev)
                prev = (s, i, g, es)
        emit_pv(*prev)
        em.phase_end()


def phase_p3(nc, em, cfg, l, dr, cst):
    L, NSEQ = cfg.L, cfg.NSEQ
    NTS, NCS, NBS = L // 128, L // 64, L // 512
    with ExitStack() as st:
        def sb(name, shape, dt):
            return st.enter_context(nc.sbuf_tensor("%s_L%d" % (name, l), shape, dt))

        def pst(name, shape, dt):
            return st.enter_context(nc.psum_tensor("%s_L%d" % (name, l), shape, dt))
        ident = cst["ident"]
        Vh = sb("p3_vh", [128, NTS, 512], BF16)
        t_Vh = em.tile("p3vh")
        scs = sb("p3_scs", [128, NBS, 6, 4, 8], F32)
        t_scs = em.tile("p3scs")
        sc2 = sb("p3_sc2", [128, 6, 4, NCS], F32)
        t_sc2 = em.tile("p3sc2")
        rKm = Ring([(sb("p3_km%d" % i, [128, L], BF16), em.tile("p3km")) for i in range(2)])
        rKt = Ring([(sb("p3_kt%d" % i, [128, 4, 128], BF16), em.tile("p3kt")) for i in range(2)])
        Ubuf = sb("p3_u", [128, NCS, 64], F32)
        t_U = em.tile("p3u")
        Sraw = sb("p3_sraw", [128, NCS + 2, 64], F32)
        t_Sraw = em.tile("p3sraw")
        Sr = [[sb("p3_sr%d%d" % (d, p), [128, NCS, 64], BF16) for p in range(4)] for d in range(2)]
        t_Sr = [[em.tile("p3sr") for p in range(4)] for d in range(2)]
        rQK = Ring([(sb("p3_qk%d" % i, [128, 4, 4, 128], BF16), em.tile("p3qk")) for i in range(2)])
        rV64 = Ring([(sb("p3_v64%d" % i, [64, 2, 512], BF16), em.tile("p3v64")) for i in range(2)])
        rG64 = Ring([(sb("p3_g64%d" % i, [64, 2, 512], BF16), em.tile("p3g64")) for i in range(2)])
        rATm = Ring([(sb("p3_atm%d" % i, [64, 2, 8, 64], BF16), em.tile("p3atm")) for i in range(2)])
        rSq = Ring([(sb("p3_sq%d" % i, [64, 8, 64], F32), em.tile("p3sq")) for i in range(2)])
        rOn = Ring([(sb("p3_on%d" % i, [64, 8, 64], F32), em.tile("p3on")) for i in range(2)])
        rSs = Ring([(sb("p3_ss%d" % i, [64, 8], F32), em.tile("p3ss")) for i in range(2)])
        rOh = Ring([(sb("p3_oh%d" % i, [64, 512], BF16), em.tile("p3oh")) for i in range(2)])
        rOT = Ring([(sb("p3_ot%d" % i, [128, 4, 128], BF16), em.tile("p3ot")) for i in range(2)])
        pTk = pst("p3_ptk", [128, 1024], BF16)
        t_pTk = em.tile("p3ptk")
        pU = [pst("p3_pu%d" % i, [128, 4, 128], F32) for i in range(2)]
        t_pU = em.tiles("p3pu", 2)
        pATf = [pst("p3_pat%d" % i, [128, 512], F32) for i in range(2)]
        pAT = [x[0:64, :] for x in pATf]
        t_pAT = em.tiles("p3pat", 2)
        pOf = pst("p3_po", [128, 512], F32)
        pO = pOf[0:64, :]
        t_pO = em.tile("p3po")
        pObf = pst("p3_pob", [128, 512], F32)
        pOb = pObf[0:64, :]
        t_pOb = em.tile("p3pob")
        usets = [([pU[0], pU[1]], t_pU),
                 ([x[:].rearrange("q (a b) -> q a b", b=128) for x in pATf], t_pAT),
                 ([pOf[:].rearrange("q (a b) -> q a b", b=128), pObf[:].rearrange("q (a b) -> q a b", b=128)], [t_pO, t_pOb])]
        ucount = [0]
        rOs = Ring([(sb("p3_os%d" % i, [64, 8, 64], F32), em.tile("p3os")) for i in range(2)])

        def load_ab(s):
            t0 = s * L
            em.dma("sp", lambda e: e.dma_start(
                out=Vh[:], in_=dr["HI"][t0:t0 + L, :].rearrange("(i p) f -> p i f", p=128)), writes=[t_Vh])
            em.dma("sp", lambda e: e.dma_start(
                out=scs[:], in_=dr["SC"][s * NBS:(s + 1) * NBS].rearrange("b q i p c -> q b i p c")),
                writes=[t_scs])
            for idx in range(6):
                em.op("pool", lambda e, idx=idx: e.tensor_copy(
                    out=sc2[:, idx, :, :].rearrange("q p (b c) -> q p b c", c=8),
                    in_=scs[:, :, idx, :, :].rearrange("q b p c -> q p b c")), reads=[t_scs], writes=[t_sc2])
            em.op("pool", lambda e: e.memset(sc2[:, 0, :, 0:1], 0.0), writes=[t_sc2])
            em.op("pool", lambda e: e.memset(sc2[:, 3, :, NCS - 1:NCS], 0.0), writes=[t_sc2])

        def load_km(s, dp):
            d, p = divmod(dp, 4)
            kmsrc = dr["KMF" if d == 0 else "KMB"]
            km, tkm = rKm.next()
            em.dma("sp", lambda e: e.dma_start(
                out=km[:], in_=kmsrc[p * 128:(p + 1) * 128, s * L:(s + 1) * L]), writes=[tkm])
            return km, tkm

        def load_c(s, i):
            tau = s * L + i * 128
            qk, tqk = rQK.next()
            for a, nm in enumerate(("QMF", "KMF", "QMB", "KMB")):
                em.dma("sp", lambda e, a=a, nm=nm: e.dma_start(
                    out=qk[:, a, :, :], in_=dr[nm][:, tau:tau + 128].rearrange("(p q) t -> q p t", q=128)),
                    writes=[tqk])
            v64, tv64 = rV64.next()
            g64, tg64 = rG64.next()
            em.dma("sp", lambda e: e.dma_start(
                out=v64[:], in_=dr["HI"][tau:tau + 128, :].rearrange("(c s) f -> s c f", s=64)), writes=[tv64])
            em.dma("sp", lambda e: e.dma_start(
                out=g64[:], in_=dr["SHG"][tau:tau + 128, :].rearrange("(c s) f -> s c f", s=64)), writes=[tg64])
            return qk, tqk, v64, tv64, g64, tg64

        def emit_tr(km, grp):
            kt, tkt = rKt.next()

            def ftr(pe):
                for n in range(4):
                    i = grp * 4 + n
                    ins = pe.transpose(out=pTk[:, n * 128:(n + 1) * 128],
                                       in_=km[0][:, i * 128:(i + 1) * 128], identity=ident[:])
                return ins
            em.op("pe", ftr, reads=[km[1]], writes=[t_pTk])
            em.op("act", lambda e: e.activation(
                out=kt[:].rearrange("q n f -> q (n f)"), in_=pTk[:, 0:512], func=AF.Copy),
                reads=[t_pTk], writes=[tkt])
            return kt, tkt

        def emit_u(d, p, grp, kt, tkt):
            pU_, t_pU_ = usets[ucount[0] % 3]
            ucount[0] += 1

            def fu(pe):
                for n in range(4):
                    i = grp * 4 + n
                    for hh in range(2):
                        ins = pe.matmul(pU_[hh][:, n, :],
                                        lhsT=kt[hh * 64:(hh + 1) * 64, n, :],
                                        rhs=Vh[hh * 64:(hh + 1) * 64, i, p * 128:(p + 1) * 128],
                                        start=True, stop=True)
                return ins
            em.op("pe", fu, reads=[tkt, t_Vh], writes=t_pU_)
            c0 = grp * 8
            for hh in range(2):
                for h2 in range(2):
                    rows = slice(h2 * 64, (h2 + 1) * 64)
                    em.op("dve", lambda e, rows=rows, h2=h2, hh=hh: e.tensor_tensor(
                        out=Ubuf[rows, c0 + hh:c0 + 8:2, :], in0=pU_[hh][rows, :, h2 * 64:(h2 + 1) * 64],
                        in1=bc(sc2[rows, 3 * d + 1, p, c0 + hh:c0 + 8:2].unsqueeze(2), [64, 4, 64]),
                        op=ALU.mult), reads=[t_pU_[hh], t_sc2], writes=[t_U])

        def emit_scan(d, p):
            zslot = 1 if d == 0 else NCS
            em.op("dve", lambda e: e.memset(Sraw[:, zslot, :], 0.0), writes=[t_Sraw])
            if NCS > 1:
                for e_ in range(64):
                    if d == 0:
                        o_ap, a_ap, u_ap = (Sraw[:, 2:NCS + 1, e_], sc2[:, 0, p, 0:NCS - 1],
                                            Ubuf[:, 0:NCS - 1, e_])
                    else:
                        o_ap = Sraw[:, NCS - 1:0:-1, e_]
                        a_ap = sc2[:, 3, p, NCS - 1:0:-1]
                        u_ap = Ubuf[:, NCS - 1:0:-1, e_]
                    em.op("dve", lambda e, o_ap=o_ap, a_ap=a_ap, u_ap=u_ap: e.tensor_tensor_scan(
                        out=o_ap, data0=a_ap, data1=u_ap, initial=0.0, op0=ALU.mult, op1=ALU.add),
                        reads=[t_U, t_sc2], writes=[t_Sraw])
            em.op("pool", lambda e: e.tensor_tensor(
                out=Sr[d][p][:], in0=Sraw[:, 1:NCS + 1, :],
                in1=bc(sc2[:, 3 * d + 2, p, :].unsqueeze(2), [128, NCS, 64]), op=ALU.mult),
                reads=[t_Sraw, t_sc2], writes=[t_Sr[d][p]])

        NG4 = NTS // 4
        load_ab(0)
        kms = {0: load_km(0, 0)}
        for s in range(NSEQ):
            t0 = s * L
            items = [(dp, grp) for dp in range(8) for grp in range(NG4)]
            if 1 not in kms:
                kms[1] = load_km(s, 1)
            ktn = emit_tr(kms[0], 0)
            for n, (dp, grp) in enumerate(items):
                d, p = divmod(dp, 4)
                kt, tkt = ktn
                if n + 1 < len(items):
                    dpn, grpn = items[n + 1]
                    if grpn == 0 and dpn + 1 < 8:
                        kms[dpn + 1] = load_km(s, dpn + 1)
                    ktn = emit_tr(kms[dpn], grpn)
                emit_u(d, p, grp, kt, tkt)
                if grp == NG4 - 1:
                    emit_scan(d, p)
            kms = {}
            allSr = [t_Sr[d][p] for d in range(2) for p in range(4)]

            def emit_tail(oh, toh, oT, toT, tok, tau, cc):
                def ft(pe):
                    for f in range(4):
                        ins = pe.transpose(out=pTk[:, f * 64:(f + 1) * 64], in_=oh[:, f * 128:(f + 1) * 128],
                                           identity=ident[0:64, 0:64])
                    return ins
                em.op("pe", ft, reads=[toh], writes=[t_pTk])
                em.op("act", lambda e: e.activation(
                    out=oT[:, :, tok], in_=pTk[:, 0:256].rearrange("q (f t) -> q f t", t=64), func=AF.Copy),
                    reads=[t_pTk], writes=[toT])
                if cc == 1:
                    em.dma("sp", lambda e: e.dma_start(
                        out=dr["OHGT"][:, tau:tau + 128].rearrange("(f q) t -> q f t", q=128), in_=oT[:]), reads=[toT])
            pend = None
            patsets = [(pAT, t_pAT),
                       ([pU[0][0:64, :, :].rearrange("s a b -> s (a b)"), pU[1][0:64, :, :].rearrange("s a b -> s (a b)")], t_pU)]

            def emit_at(qk, tqk, tok, cidx):
                pa, tpa = patsets[cidx % 2]

                def fat(pe):
                    for d in range(2):
                        for p in range(4):
                            for h2 in range(2):
                                rows = slice(h2 * 64, (h2 + 1) * 64)
                                slot = d * 4 + p
                                ins = pe.matmul(pa[h2][:, slot * 64:(slot + 1) * 64],
                                                lhsT=qk[rows, 2 * d + 1, p, tok], rhs=qk[rows, 2 * d, p, tok],
                                                start=True, stop=True)
                    return ins
                em.op("pe", fat, reads=[tqk], writes=tpa)
                atm, tatm = rATm.next()
                for h2 in range(2):
                    em.op("dve", lambda e, h2=h2: e.tensor_tensor(
                        out=atm[:, :, h2:8:2, :],
                        in0=pa[h2][:, :].rearrange("s (d p t) -> s d p t", d=2, t=64),
                        in1=bc(cst["tri"][:, :, :].unsqueeze(2), [64, 2, 4, 64]), op=ALU.mult),
                        reads=[tpa[h2]], writes=[tatm])
                return atm, tatm
            atnext = None
            cnext = load_c(s, 0)
            if s + 1 < NSEQ:
                load_ab(s + 1)
                kms[0] = load_km(s + 1, 0)
            for i in range(NTS):
                tau = t0 + i * 128
                qk, tqk, v64, tv64, g64, tg64 = cnext
                if i + 1 < NTS:
                    cnext = load_c(s, i + 1)
                oT, toT = rOT.next()
                for cc in range(2):
                    c = i * 2 + cc
                    tok = slice(cc * 64, (cc + 1) * 64)
                    if atnext is None:
                        atnext = emit_at(qk, tqk, tok, c)
                    atm, tatm = atnext
                    if cc == 0:
                        atnext = emit_at(qk, tqk, slice(64, 128), c + 1)
                    elif i + 1 < NTS:
                        atnext = emit_at(cnext[0], cnext[1], slice(0, 64), c + 1)
                    else:
                        atnext = None

                    def fo(pe, atm=atm, qk=qk, v64=v64, cc=cc, c=c, tok=tok):
                        for p in range(4):
                            for h2 in range(2):
                                hh = p * 2 + h2
                                rows = slice(h2 * 64, (h2 + 1) * 64)
                                o_ap = pO[:, hh * 64:(hh + 1) * 64]
                                ob_ap = pOb[:, hh * 64:(hh + 1) * 64]
                                if h2 == 0:
                                    for d in range(2):
                                        pe.matmul(o_ap, lhsT=atm[:, d, hh, :], rhs=v64[:, cc, hh * 64:(hh + 1) * 64],
                                                  start=(d == 0), stop=False)
                                        ins = pe.matmul(o_ap, lhsT=qk[rows, 2 * d, p, tok], rhs=Sr[d][p][rows, c, :],
                                                        start=False, stop=(d == 1))
                                else:
                                    for d in range(2):
                                        pe.matmul(o_ap, lhsT=atm[:, d, hh, :], rhs=v64[:, cc, hh * 64:(hh + 1) * 64],
                                                  start=(d == 0), stop=(d == 1))
                                        ins = pe.matmul(ob_ap, lhsT=qk[rows, 2 * d, p, tok], rhs=Sr[d][p][rows, c, :],
                                                        start=(d == 0), stop=(d == 1))
                        return ins
                    em.op("pe", fo, reads=[tatm, tqk, tv64] + allSr, writes=[t_pO, t_pOb])
                    sq, tsq = rSq.next()
                    on, ton = rOn.next()
                    ss, tss = rSs.next()
                    oh, toh = rOh.next()
                    osb, tos = rOs.next()
                    em.op("act", lambda e, osb=osb: e.activation(
                        out=osb[:].rearrange("t h e -> t (h e)"), in_=pO[:], func=AF.Copy),
                        reads=[t_pO], writes=[tos])
                    em.op("dve", lambda e, osb=osb: e.tensor_tensor(
                        out=osb[:, 1:8:2, :], in0=pOb[:].rearrange("t (h e) -> t h e", e=64)[:, 1:8:2, :],
                        in1=osb[:, 1:8:2, :], op=ALU.add), reads=[t_pOb, tos], writes=[tos])
                    pov = osb[:]
                    em.op("act", lambda e, sq=sq, osb=osb: e.activation(
                        out=sq[:].rearrange("t h e -> t (h e)"), in_=osb[:].rearrange("t h e -> t (h e)"),
                        func=AF.Square), reads=[tos], writes=[tsq])
                    em.op("dve", lambda e, sq=sq, ss=ss: e.tensor_reduce(
                        out=ss[:], in_=sq[:], axis=AX.X, op=ALU.add), reads=[tsq], writes=[tss])
                    em.op("act", lambda e, ss=ss: e.activation(
                        out=ss[:], in_=ss[:], func=AF.Sqrt, scale=1.0 / 64, bias=cst["eps_rms"][0:64, :]),
                        reads=[tss], writes=[tss])
                    em.op("dve", lambda e, ss=ss: e.reciprocal(out=ss[:], in_=ss[:]), reads=[tss], writes=[tss])
                    em.op("dve", lambda e, on=on, ss=ss, pov=pov: e.tensor_tensor(
                        out=on[:], in0=pov, in1=bc(ss[:].unsqueeze(2), [64, 8, 64]), op=ALU.mult),
                        reads=[tos, tss], writes=[ton])
                    em.op("pool", lambda e, on=on: e.tensor_tensor(
                        out=on[:], in0=on[:], in1=bc(cst["normg"][:, l, :].unsqueeze(1), [64, 8, 64]),
                        op=ALU.mult), reads=[ton], writes=[ton])
                    em.op("pool", lambda e, on=on, oh=oh, g64=g64, cc=cc: e.tensor_tensor(
                        out=oh[:], in0=on[:].rearrange("t h e -> t (h e)"), in1=g64[:, cc, :], op=ALU.mult),
                        reads=[ton, tg64], writes=[toh])

                    if pend is not None:
                        emit_tail(*pend)
                    pend = (oh, toh, oT, toT, tok, tau, cc)
            if pend is not None:
                emit_tail(*pend)
                pend = None
        em.phase_end()


def layer_norm_tile(nc, em, cst, pre, tpre, xn, txn, stats, tstats, gam, bet, t_gb):
    for hh in range(2):
        em.op("dve", lambda e, hh=hh: e.bn_stats(out=stats[:, hh, :], in_=pre[:, hh * 512:(hh + 1) * 512]),
              reads=[tpre], writes=[tstats[hh]])
    em.op("dve", lambda e: e.bn_aggr(out=stats[:, 2, 0:2], in_=stats[:, 0:2, :].rearrange("p a b -> p (a b)")),
          reads=tstats[0:2], writes=[tstats[2]])
    em.op("act", lambda e: e.activation(out=stats[:, 2, 2:3], in_=stats[:, 2, 1:2], func=AF.Sqrt,
                                        bias=cst["eps_ln"][:], scale=1.0), reads=[tstats[2]], writes=[tstats[3]])
    em.op("dve", lambda e: e.reciprocal(out=stats[:, 2, 3:4], in_=stats[:, 2, 2:3]), reads=[tstats[3]], writes=[tstats[4]])
    em.op("dve", lambda e: e.scalar_tensor_tensor(out=xn[:], in0=pre[:], scalar=stats[:, 2, 0:1], in1=gam[:],
                                                  op0=ALU.subtract, op1=ALU.mult), reads=[tpre, tstats[2], t_gb], writes=[txn])
    em.op("dve", lambda e: e.scalar_tensor_tensor(out=xn[:], in0=xn[:], scalar=stats[:, 2, 3:4], in1=bet[:],
                                                  op0=ALU.mult, op1=ALU.add), reads=[txn, tstats[4], t_gb], writes=[txn])


def phase_p5a(nc, em, cfg, l, dr, cst, xsrc):
    NB = cfg.NB
    with ExitStack() as st:
        def sb(name, shape, dt):
            return st.enter_context(nc.sbuf_tensor("%s_L%d" % (name, l), shape, dt))

        def pst(name, shape, dt):
            return st.enter_context(nc.psum_tensor("%s_L%d" % (name, l), shape, dt))
        ident = cst["ident"]
        Wpa = sb("a_wpa", [128, 4, D], BF16)
        Wph = sb("a_wph", [128, 4, D], BF16)
        Wo = sb("a_wo", [128, 8, D], BF16)
        gam = sb("a_gam", [128, D], F32)
        bet = sb("a_bet", [128, D], F32)
        t_w = em.tile("a_w")
        t_gb = em.tile("a_gb")
        em.dma("pool", lambda e: e.dma_start(out=Wpa[:], in_=dr["w_proj_att"][l].rearrange("(k q) n -> q k n", q=128)), writes=[t_w])
        em.dma("pool", lambda e: e.dma_start(out=Wph[:], in_=dr["w_proj_hgrn"][l].rearrange("(k q) n -> q k n", q=128)), writes=[t_w])
        for k in range(8):
            em.dma("pool", lambda e, k=k: e.dma_start(out=Wo[:, k, :], in_=dr["w_out"][l][k * 128:(k + 1) * 128, :]), writes=[t_w])
        em.dma("sp", lambda e: e.dma_start(out=gam[:], in_=dr["ln1_g"][l].partition_broadcast(128)), writes=[t_gb])
        em.dma("sp", lambda e: e.dma_start(out=bet[:], in_=dr["ln1_b"][l].partition_broadcast(128)), writes=[t_gb])
        rOa = Ring([(sb("a_oa%d" % i, [128, 4, 512], BF16), em.tile("a_oa")) for i in range(2)])
        rOh = Ring([(sb("a_oh%d" % i, [128, 4, 512], BF16), em.tile("a_oh")) for i in range(2)])
        rGa = Ring([(sb("a_ga%d" % i, [128, 8, 512], BF16), em.tile("a_ga")) for i in range(2)])
        rGb = Ring([(sb("a_gb%d" % i, [128, 8, 512], BF16), em.tile("a_gb")) for i in range(2)])
        rX = Ring([(sb("a_x%d" % i, [128, D], F32), em.tile("a_x")) for i in range(3)])
        mixT = sb("a_mix", [128, 8, 512], BF16)
        t_mix = em.tiles("a_mix", 8)
        rM1 = Ring([(sb("a_m1%d" % i, [128, 512], F32), em.tile("a_m1")) for i in range(2)])
        rM2 = Ring([(sb("a_m2%d" % i, [128, 512], F32), em.tile("a_m2")) for i in range(2)])
        rPre = Ring([(sb("a_pre%d" % i, [128, D], F32), em.tile("a_pre")) for i in range(2)])
        rXn = Ring([(sb("a_xn%d" % i, [128, D], F32), em.tile("a_xn")) for i in range(2)])
        rSt = Ring([(sb("a_st%d" % i, [128, 3, 6], F32), em.tiles("a_st", 5)) for i in range(2)])
        rXb = Ring([(sb("a_xb%d" % i, [128, D], BF16), em.tile("a_xb")) for i in range(2)])
        rXT = Ring([(sb("a_xT%d" % i, [128, 8, 128], BF16), em.tile("a_xT")) for i in range(2)])
        rP = Ring([(pst("a_ps%d" % i, [128, 512], F32), em.tile("a_ps")) for i in range(6)])
        rPT = Ring([(pst("a_pt%d" % i, [128, 1024], BF16), em.tile("a_pt")) for i in range(2)])
        def load_blk(b):
            tau0 = b * 512
            oa, toa = rOa.next()
            oh, toh = rOh.next()
            ga, tga = rGa.next()
            gb, tgb = rGb.next()
            em.dma("sp", lambda e: e.dma_start(
                out=oa[:], in_=dr["OATT"][:, tau0:tau0 + 512].rearrange("(k q) t -> q k t", q=128)), writes=[toa])
            em.dma("sp", lambda e: e.dma_start(
                out=oh[:], in_=dr["OHGT"][:, tau0:tau0 + 512].rearrange("(k q) t -> q k t", q=128)), writes=[toh])
            em.dma("sp", lambda e: e.dma_start(
                out=ga[:], in_=dr["SGA"][:, tau0:tau0 + 512].rearrange("(k q) t -> q k t", q=128)), writes=[tga])
            em.dma("sp", lambda e: e.dma_start(
                out=gb[:], in_=dr["SGB"][:, tau0:tau0 + 512].rearrange("(k q) t -> q k t", q=128)), writes=[tgb])
            return oa, toa, oh, toh, ga, tga, gb, tgb

        def load_x(n):
            xt, txt = rX.next()
            em.dma("sp", lambda e: e.dma_start(out=xt[:], in_=xsrc[n * 128:(n + 1) * 128, :]), writes=[txt])
            return xt, txt

        def emit_tr(xb, txb, tau):
            pT, tpT = rPT.next()

            def ftr(pe):
                for k in range(8):
                    ins = pe.transpose(out=pT[:, k * 128:(k + 1) * 128], in_=xb[:, k * 128:(k + 1) * 128], identity=ident[:])
                return ins
            em.op("pe", ftr, reads=[txb], writes=[tpT])
            xT, txT = rXT.next()
            em.op("act", lambda e: e.activation(
                out=xT[:].rearrange("q k t -> q (k t)"), in_=pT[:], func=AF.Copy), reads=[tpT], writes=[txT])
            em.dma("sp", lambda e: e.dma_start(
                out=dr["X1T"][:, tau:tau + 128].rearrange("(k q) t -> q k t", q=128), in_=xT[:]), reads=[txT])

        pend = None
        bnext = load_blk(0)
        xnext = load_x(0)
        for b in range(NB):
            tau0 = b * 512
            oa, toa, oh, toh, ga, tga, gb, tgb = bnext
            if b + 1 < NB:
                bnext = load_blk(b + 1)
            for ft in range(8):
                cols = slice(ft * 128, (ft + 1) * 128)
                p1, tp1 = rP.next()

                def fa(pe, p1=p1, cols=cols):
                    for h in range(4):
                        ins = pe.matmul(p1[:], lhsT=Wpa[:, h, cols], rhs=oa[:, h, :], start=(h == 0), stop=(h == 3))
                    return ins
                em.op("pe", fa, reads=[t_w, toa], writes=[tp1])
                m1, tm1 = rM1.next()
                em.op("dve", lambda e, m1=m1, p1=p1, ft=ft: e.tensor_tensor(
                    out=m1[:], in0=p1[:], in1=ga[:, ft, :], op=ALU.mult), reads=[tp1, tga], writes=[tm1])
                p2, tp2 = rP.next()

                def fh(pe, p2=p2, cols=cols):
                    for k in range(4):
                        ins = pe.matmul(p2[:], lhsT=Wph[:, k, cols], rhs=oh[:, k, :], start=(k == 0), stop=(k == 3))
                    return ins
                em.op("pe", fh, reads=[t_w, toh], writes=[tp2])
                m2, tm2 = rM2.next()
                em.op("dve", lambda e, m2=m2, p2=p2, ft=ft: e.tensor_tensor(
                    out=m2[:], in0=p2[:], in1=gb[:, ft, :], op=ALU.mult), reads=[tp2, tgb], writes=[tm2])
                em.op("dve", lambda e, m1=m1, m2=m2, ft=ft: e.tensor_tensor(
                    out=mixT[:, ft, :], in0=m1[:], in1=m2[:], op=ALU.add), reads=[tm1, tm2], writes=[t_mix[ft]])
            for tt in range(4):
                tau = tau0 + tt * 128
                xt, txt = xnext
                if b * 4 + tt + 1 < NB * 4:
                    xnext = load_x(b * 4 + tt + 1)
                pre, tpre = rPre.next()
                for hh in range(2):
                    p3, tp3 = rP.next()

                    def fw(pe, p3=p3, tt=tt, hh=hh):
                        for k in range(8):
                            ins = pe.matmul(p3[:], lhsT=mixT[:, k, tt * 128:(tt + 1) * 128],
                                            rhs=Wo[:, k, hh * 512:(hh + 1) * 512], start=(k == 0), stop=(k == 7))
                        return ins
                    em.op("pe", fw, reads=[t_w] + t_mix, writes=[tp3])
                    em.op("dve", lambda e, pre=pre, xt=xt, p3=p3, hh=hh: e.scalar_tensor_tensor(
                        out=pre[:, hh * 512:(hh + 1) * 512], in0=xt[:, hh * 512:(hh + 1) * 512], scalar=ALPHA,
                        in1=p3[:], op0=ALU.mult, op1=ALU.add), reads=[txt, tp3], writes=[tpre])
                xn, txn = rXn.next()
                stt, tstt = rSt.next()
                layer_norm_tile(nc, em, cst, pre, tpre, xn, txn, stt, tstt, gam, bet, t_gb)
                em.dma("sp", lambda e, xn=xn, tau=tau: e.dma_start(out=dr["X1"][tau:tau + 128, :], in_=xn[:]), reads=[txn])
                xb, txb = rXb.next()
                em.op("act", lambda e, xb=xb, xn=xn: e.activation(out=xb[:], in_=xn[:], func=AF.Copy), reads=[txn], writes=[txb])
                if pend is not None:
                    emit_tr(*pend)
                pend = (xb, txb, tau)
        emit_tr(*pend)
        em.phase_end()


def phase_p5b(nc, em, cfg, l, dr, cst, dst):
    NB = cfg.NB
    with ExitStack() as st:
        def sb(name, shape, dt):
            return st.enter_context(nc.sbuf_tensor("%s_L%d" % (name, l), shape, dt))

        def pst(name, shape, dt):
            return st.enter_context(nc.psum_tensor("%s_L%d" % (name, l), shape, dt))
        W2 = sb("b_w2", [128, 32, D], BF16)
        t_w2 = [em.tile("b_w2")]
        for k in range(32):
            em.dma("pool", lambda e, k=k: e.dma_start(out=W2[:, k, :], in_=dr["w_ff2"][l][k * 128:(k + 1) * 128, :]),
                   writes=t_w2)
        gam = sb("b_gam", [128, D], F32)
        bet = sb("b_bet", [128, D], F32)
        t_gb = em.tile("b_gb")
        em.dma("sp", lambda e: e.dma_start(out=gam[:], in_=dr["ln2_g"][l].partition_broadcast(128)), writes=[t_gb])
        em.dma("sp", lambda e: e.dma_start(out=bet[:], in_=dr["ln2_b"][l].partition_broadcast(128)), writes=[t_gb])
        rW1 = Ring([(sb("b_w1%d" % i, [128, 8, 512], BF16), em.tile("b_w1")) for i in range(4)])
        rXT = Ring([(sb("b_xT%d" % i, [128, 8, 512], BF16), em.tile("b_xT")) for i in range(2)])
        rX = Ring([(sb("b_x%d" % i, [128, D], F32), em.tile("b_x")) for i in range(3)])
        hT = sb("b_hT", [128, 32, 512], BF16)
        t_hT = em.tiles("b_hT", 32)
        rR = Ring([(sb("b_r%d" % i, [128, 512], F32), em.tile("b_r")) for i in range(3)])
        rPre = Ring([(sb("b_pre%d" % i, [128, D], F32), em.tile("b_pre")) for i in range(2)])
        rXn = Ring([(sb("b_xn%d" % i, [128, D], F32), em.tile("b_xn")) for i in range(2)])
        rSt = Ring([(sb("b_st%d" % i, [128, 3, 6], F32), em.tiles("b_st", 5)) for i in range(2)])
        rP = Ring([(pst("b_ps%d" % i, [128, 512], F32), em.tile("b_ps")) for i in range(4)])
        rP2 = Ring([(pst("b_pq%d" % i, [128, 512], F32), em.tile("b_pq")) for i in range(4)])
        w1v = dr["W1B"][l].rearrange("(k q) n -> q k n", q=128)
        def load_xT(b):
            xT, txT = rXT.next()
            em.dma("sp", lambda e: e.dma_start(
                out=xT[:], in_=dr["X1T"][:, b * 512:(b + 1) * 512].rearrange("(k q) t -> q k t", q=128)), writes=[txT])
            return xT, txT

        def load_w1(n):
            j = n % 8
            w1, tw1 = rW1.next()
            em.dma("sp", lambda e: e.dma_start(out=w1[:], in_=w1v[:, :, j * 512:(j + 1) * 512]), writes=[tw1])
            return w1, tw1

        def load_x(n):
            xt, txt = rX.next()
            em.dma("sp", lambda e: e.dma_start(out=xt[:], in_=dr["X1"][n * 128:(n + 1) * 128, :]), writes=[txt])
            return xt, txt

        xTnext = load_xT(0)
        w1q = [load_w1(0), load_w1(1), load_w1(2)]
        xnext = load_x(0)
        for b in range(NB):
            tau0 = b * 512
            xT, txT = xTnext
            if b + 1 < NB:
                xTnext = load_xT(b + 1)
            for j in range(8):
                w1, tw1 = w1q.pop(0)
                if b * 8 + j + 3 < NB * 8:
                    w1q.append(load_w1(b * 8 + j + 3))
                for f4 in range(4):
                    ft = j * 4 + f4
                    p1, tp1 = rP.next()

                    def f1(pe, p1=p1, w1=w1, f4=f4):
                        for k in range(8):
                            ins = pe.matmul(p1[:], lhsT=w1[:, k, f4 * 128:(f4 + 1) * 128], rhs=xT[:, k, :],
                                            start=(k == 0), stop=(k == 7))
                        return ins
                    em.op("pe", f1, reads=[tw1, txT], writes=[tp1])
                    r_, tr_ = rR.next()
                    em.op("act", lambda e, r_=r_, p1=p1: e.activation(out=r_[:], in_=p1[:], func=AF.Relu),
                          reads=[tp1], writes=[tr_])
                    em.op("pool", lambda e, r_=r_, ft=ft: e.tensor_tensor(
                        out=hT[:, ft, :], in0=r_[:], in1=r_[:], op=ALU.mult), reads=[tr_], writes=[t_hT[ft]])
            for tt in range(4):
                tau = tau0 + tt * 128
                xt, txt = xnext
                if b * 4 + tt + 1 < NB * 4:
                    xnext = load_x(b * 4 + tt + 1)
                pre, tpre = rPre.next()
                for hh in range(2):
                    p2, tp2 = rP2.next()

                    def f2(pe, p2=p2, tt=tt, hh=hh):
                        for k in range(32):
                            ins = pe.matmul(p2[:], lhsT=hT[:, k, tt * 128:(tt + 1) * 128],
                                            rhs=W2[:, k, hh * 512:(hh + 1) * 512], start=(k == 0), stop=(k == 31))
                        return ins
                    em.op("pe", f2, reads=t_hT + t_w2, writes=[tp2])
                    em.op("dve", lambda e, pre=pre, xt=xt, p2=p2, hh=hh: e.scalar_tensor_tensor(
                        out=pre[:, hh * 512:(hh + 1) * 512], in0=xt[:, hh * 512:(hh + 1) * 512], scalar=ALPHA,
                        in1=p2[:], op0=ALU.mult, op1=ALU.add), reads=[txt, tp2], writes=[tpre])
                xn, txn = rXn.next()
                stt, tstt = rSt.next()
                layer_norm_tile(nc, em, cst, pre, tpre, xn, txn, stt, tstt, gam, bet, t_gb)
                em.dma("sp", lambda e, xn=xn, tau=tau: e.dma_start(out=dst[tau:tau + 128, :], in_=xn[:]), reads=[txn])
        em.phase_end()

def declare_io(nc, cfg):
    T = cfg.T
    dr = {}

    def inp(name, shape):
        dr[name] = nc.dram_tensor(name, list(shape), F32, kind="ExternalInput").ap()
    inp("x", [T, D])
    inp("w_in", [2, D, DIN])
    inp("att_sink", [2, 8])
    inp("hgrn_lb", [2, 2, 512])
    inp("hgrn_norm_g", [2, 64])
    inp("w_proj_att", [2, 512, D])
    inp("w_proj_hgrn", [2, 512, D])
    inp("w_out", [2, D, D])
    inp("ln1_g", [2, D])
    inp("ln1_b", [2, D])
    inp("w_ff1", [2, D, DFF])
    inp("w_ff2", [2, DFF, D])
    inp("ln2_g", [2, D])
    inp("ln2_b", [2, D])
    inp("c_ident", [128, 128])
    inp("c_cos", [cfg.L, 8])
    inp("c_sin", [cfg.L, 8])
    inp("c_cmask", [128, 512])
    inp("c_tri", [2, 64, 64])
    inp("c_band", [2, 128, 128])
    dr["y"] = nc.dram_tensor("y", [T, D], F32, kind="ExternalOutput").ap()
    kind = "ExternalOutput" if cfg.debug else "Internal"

    def scr(name, shape, dt):
        dr[name] = nc.dram_tensor(name, list(shape), dt, kind=kind).ap()
    scr("QT", [64, 8, T], BF16)
    scr("KT", [64, 2, T], BF16)
    scr("V", [T, 128], BF16)
    scr("HI", [T, 512], BF16)
    scr("SHG", [T, 512], BF16)
    for n in ("QMF", "KMF", "QMB", "KMB"):
        scr(n, [512, T], BF16)
    scr("SC", [cfg.NB, 128, 6, 4, 8], F32)
    scr("SGA", [D, T], BF16)
    scr("SGB", [D, T], BF16)
    scr("OATT", [512, T], BF16)
    scr("OHGT", [512, T], BF16)
    scr("X1", [T, D], F32)
    scr("X1T", [D, T], BF16)
    scr("XL", [T, D], F32)
    scr("W1B", [2, D, DFF], BF16)
    return dr


def setup_consts(nc, em, cfg, dr, st):
    def sb(name, shape, dt):
        return st.enter_context(nc.sbuf_tensor(name, shape, dt))
    cst = {}
    NTS = cfg.L // 128
    cst["ident"] = sb("k_ident", [128, 128], BF16)
    cst["cos"] = sb("k_cos", [128, NTS, 8], F32)
    cst["sin"] = sb("k_sin", [128, NTS, 8], F32)
    cst["cmask"] = sb("k_cmask", [128, 512], F32)
    cst["tri"] = sb("k_tri", [64, 2, 64], F32)
    cst["band"] = sb("k_band", [128, 2, 128], BF16)
    cst["ones"] = sb("k_ones", [128, 64], BF16)
    cst["lbt"] = sb("k_lbt", [128, 16], F32)
    cst["oml"] = sb("k_oml", [128, 16], F32)
    cst["esink"] = sb("k_esink", [64, 16], F32)
    cst["normg"] = sb("k_normg", [64, 2, 64], F32)
    cst["eps_ln"] = sb("k_epsln", [128, 1], F32)
    cst["eps_rms"] = sb("k_epsrms", [128, 1], F32)
    raw = sb("k_lbraw", [128, 16], F32)
    tmp = sb("k_lbtmp", [128, 4, 8], F32)
    t = em.tile("consts")
    traw = em.tile("lbraw")
    em.dma("pool", lambda e: e.dma_start(out=cst["ident"][:], in_=dr["c_ident"]), writes=[t])
    em.dma("sp", lambda e: e.dma_start(out=cst["cos"][:], in_=dr["c_cos"].rearrange("(i p) f -> p i f", p=128)), writes=[t])
    em.dma("sp", lambda e: e.dma_start(out=cst["sin"][:], in_=dr["c_sin"].rearrange("(i p) f -> p i f", p=128)), writes=[t])
    em.dma("sp", lambda e: e.dma_start(out=cst["cmask"][:], in_=dr["c_cmask"]), writes=[t])
    em.dma("sp", lambda e: e.dma_start(out=cst["tri"][:], in_=dr["c_tri"].rearrange("d s t -> s d t")), writes=[t])
    em.dma("pool", lambda e: e.dma_start(out=cst["band"][:], in_=dr["c_band"].rearrange("d s t -> s d t")), writes=[t])
    with nc.allow_non_contiguous_dma("tiny parameter gathers"):
        em.dma("sp", lambda e: e.dma_start(
            out=raw[:].rearrange("q (a p) -> q a p", p=4),
            in_=dr["hgrn_lb"].rearrange("l d (p q) -> q (l d) p", q=128)), writes=[traw])
    em.dma("sp", lambda e: e.dma_start(
        out=cst["esink"][:], in_=dr["att_sink"].rearrange("l h -> (l h)").partition_broadcast(64)), writes=[t])
    em.dma("sp", lambda e: e.dma_start(
        out=cst["normg"][:].rearrange("p l e -> p (l e)"),
        in_=dr["hgrn_norm_g"].rearrange("l e -> (l e)").partition_broadcast(64)), writes=[t])
    em.op("dve", lambda e: e.memset(cst["ones"][:], 1.0), writes=[t])
    em.op("dve", lambda e: e.memset(cst["eps_ln"][:], LN_EPS), writes=[t])
    em.op("dve", lambda e: e.memset(cst["eps_rms"][:], RMS_EPS), writes=[t])
    e0, e1, den, rec = tmp[:, 0, :], tmp[:, 1, :], tmp[:, 2, :], tmp[:, 3, :]
    em.op("act", lambda e: e.activation(out=tmp[:, 0:2, :].rearrange("q a p -> q (a p)"), in_=raw[:], func=AF.Exp),
          reads=[traw], writes=[t])
    em.op("dve", lambda e: e.tensor_tensor(out=den, in0=e0, in1=e1, op=ALU.add), reads=[t], writes=[t])
    em.op("dve", lambda e: e.reciprocal(out=rec, in_=den), reads=[t], writes=[t])
    em.op("dve", lambda e: e.tensor_tensor(out=e0, in0=e0, in1=rec, op=ALU.mult), reads=[t], writes=[t])
    em.op("dve", lambda e: e.tensor_tensor(out=e1, in0=e1, in1=rec, op=ALU.mult), reads=[t], writes=[t])
    em.op("dve", lambda e: e.tensor_tensor(out=cst["lbt"][:, 0:8], in0=e0, in1=e0, op=ALU.subtract), reads=[t], writes=[t])
    em.op("dve", lambda e: e.tensor_tensor(out=den, in0=e0, in1=e1, op=ALU.add), reads=[t], writes=[t])
    em.op("dve", lambda e: e.tensor_tensor(out=cst["lbt"][:, 8:16], in0=den, in1=e0, op=ALU.subtract), reads=[t], writes=[t])
    em.op("dve", lambda e: e.tensor_scalar(out=cst["oml"][:], in0=cst["lbt"][:], scalar1=-1.0, scalar2=1.0,
                                           op0=ALU.mult, op1=ALU.add), reads=[t], writes=[t])
    em.op("act", lambda e: e.activation(out=cst["esink"][:], in_=cst["esink"][:], func=AF.Exp), reads=[t], writes=[t])
    tw1 = em.tile("w1b")
    for l in range(cfg.nlayer):
        for k in range(8):
            em.dma("pool", lambda e, l=l, k=k: e.dma_start(
                out=dr["W1B"][l][k * 128:(k + 1) * 128, :], in_=dr["w_ff1"][l][k * 128:(k + 1) * 128, :]), writes=[tw1])
    em.phase_end()
    return cst


def build(cfg):
    nc = bass.Bass("TRN2", target_bir_lowering=False)
    dr = declare_io(nc, cfg)
    with ExitStack() as st:
        em = Em(nc, st)
        cst = setup_consts(nc, em, cfg, dr, st)
        for l in range(cfg.nlayer):
            last = (l == cfg.nlayer - 1)
            upto = cfg.upto if last else 99
            xsrc = dr["x"] if l == 0 else dr["XL"]
            if upto > 1:
                phase_p1(nc, em, cfg, l, dr, cst, xsrc)
            if upto > 2:
                phase_p2(nc, em, cfg, l, dr, cst)
            if upto > 3:
                phase_p3(nc, em, cfg, l, dr, cst)
            if upto > 4:
                phase_p5a(nc, em, cfg, l, dr, cst, xsrc)
            if upto > 5:
                phase_p5b(nc, em, cfg, l, dr, cst, dr["y"] if last else dr["XL"])
    return nc


def host_consts(L):
    inv = (ROPE_THETA ** (-np.arange(0, 16, 2, dtype=np.float32) / 16)).astype(np.float32)
    ang = np.arange(L, dtype=np.float32)[:, None] * inv[None, :]
    cmask = np.ones((128, 512), np.float32)
    cmask[:, ::64] = 0.0
    s = np.arange(64)[:, None]
    t = np.arange(64)[None, :]
    tri = np.stack([(t >= s), (t <= s)]).astype(np.float32)
    j = np.arange(128)[:, None]
    q = np.arange(128)[None, :]
    band = np.stack([(j >= q), (j <= q)]).astype(np.float32)
    return {
        "c_ident": np.eye(128, dtype=np.float32),
        "c_cos": np.cos(ang).astype(np.float32),
        "c_sin": np.sin(ang).astype(np.float32),
        "c_cmask": cmask, "c_tri": tri, "c_band": band,
    }


WNAMES = ("w_in", "att_sink", "hgrn_lb", "hgrn_norm_g", "w_proj_att", "w_proj_hgrn", "w_out",
          "ln1_g", "ln1_b", "w_ff1", "w_ff2", "ln2_g", "ln2_b")


def kernel(**inputs):
    xp = np.asarray(inputs["x_prompt"], np.float32)
    xs = np.asarray(inputs["x_sample"], np.float32)
    L = xp.shape[1]
    allx = np.concatenate([xp, xs], axis=0)
    ncore = 8
    nseq = allx.shape[0] // ncore
    cfg = Cfg(nseq, L)
    nc = build(cfg)
    shared = {k: np.ascontiguousarray(np.asarray(inputs[k], np.float32)) for k in WNAMES}
    shared.update(host_consts(L))
    in_maps = []
    for c in range(ncore):
        m = dict(shared)
        m["x"] = np.ascontiguousarray(allx[c * nseq:(c + 1) * nseq].reshape(nseq * L, D))
        in_maps.append(m)
    res = run_bass_kernel_spmd(nc, in_maps, core_ids=list(range(ncore)))
    ys = np.stack([np.asarray(r["y"], np.float32).reshape(nseq, L, D) for r in res.results], 0)
    ys = ys.reshape(ncore * nseq, L, D)
    nb = xp.shape[0]
    return (np.ascontiguousarray(ys[:nb]), np.ascontiguousarray(ys[nb:]))
```

```python
import numpy as np
from contextlib import ExitStack
import concourse.bass as bass
import concourse.mybir as mybir
from concourse.bass_utils import run_bass_kernel_spmd

F32 = mybir.dt.float32
BF16 = mybir.dt.bfloat16
AF = mybir.ActivationFunctionType
ALU = mybir.AluOpType
AX = mybir.AxisListType

D = 1024
DIN = 5376
DFF = 4096
NLAYER = 2
ALPHA = float((2 * NLAYER) ** 0.25)
LN_EPS = 1e-5
RMS_EPS = 1e-6
ROPE_THETA = 500000.0
C_AQ, C_AK, C_AV, C_HQ, C_HFF, C_HFB, C_HI, C_HG, C_GA, C_GB = (
    0, 512, 640, 768, 1280, 1792, 2304, 2816, 3328, 4352)


class Tl:
    __slots__ = ("name", "w", "r")

    def __init__(self, name):
        self.name = name
        self.w = None
        self.r = {}


class Em:
    def __init__(self, nc, st, ndma=56):
        self.nc = nc
        self.eng = {"pe": nc.tensor, "act": nc.scalar, "dve": nc.vector,
                    "pool": nc.gpsimd, "sp": nc.sync}
        self.sem = {e: st.enter_context(nc.semaphore("s_" + e))
                    for e in ("pe", "act", "dve", "pool")}
        self.dpool = {"pool": [st.enter_context(nc.semaphore("dp%d" % i)) for i in range(16)],
                      "sp": [st.enter_context(nc.semaphore("ds%d" % i)) for i in range(40)]}
        self.dcnt = {q: [0] * len(v) for q, v in self.dpool.items()}
        self.ntile = 0
        self.reset_state()

    def reset_state(self):
        self.cnt = {k: 0 for k in self.sem}
        self.dmap = {}
        self.dused = {q: 0 for q in self.dpool}
        self.waited = {}

    def tile(self, name):
        self.ntile += 1
        return Tl("%s#%d" % (name, self.ntile))

    def tiles(self, name, n):
        return [self.tile("%s%d" % (name, i)) for i in range(n)]

    def _semh(self, k):
        return self.sem[k] if isinstance(k, str) else self.dpool[k[0]][k[1]]

    def _wait(self, engname, deps):
        best = {}
        for k, v in deps:
            if v > best.get(k, 0):
                best[k] = v
        for k, v in best.items():
            if engname == "pe" and k == "pe":
                continue
            if self.waited.get((engname, k), 0) >= v:
                continue
            self.eng[engname].wait_ge(self._semh(k), v)
            self.waited[(engname, k)] = v

    @staticmethod
    def _deps(reads, writes):
        deps = []
        for t in reads:
            if t.w:
                deps.append(t.w)
        for t in writes:
            if t.w:
                deps.append(t.w)
            deps.extend(t.r.items())
        return deps

    @staticmethod
    def _record(ev, reads, writes):
        k, v = ev
        for t in reads:
            if t.r.get(k, 0) < v:
                t.r[k] = v
        for t in writes:
            t.w = ev
            t.r = {}

    def op(self, engname, fn, reads=(), writes=()):
        self._wait(engname, self._deps(reads, writes))
        ins = fn(self.eng[engname])
        self.cnt[engname] += 1
        ins.then_inc(self.sem[engname], 1)
        self._record((engname, self.cnt[engname]), reads, writes)

    def dma(self, q, fn, reads=(), writes=(), key=None):
        self._wait(q, self._deps(reads, writes))
        kt = key or (writes[0] if writes else reads[0])
        kn = (kt.name, q)
        if kn not in self.dmap:
            self.dmap[kn] = self.dused[q]
            self.dused[q] += 1
            assert self.dused[q] <= len(self.dpool[q]), "out of dma semaphores on " + q
        i = self.dmap[kn]
        ins = fn(self.eng[q])
        self.dcnt[q][i] += 16
        ins.then_inc(self.dpool[q][i], 16)
        self._record(((q, i), self.dcnt[q][i]), reads, writes)

    def phase_end(self):
        allev = [((q, i), c) for q, v in self.dcnt.items() for i, c in enumerate(v) if c > 0]
        self._wait("sp", allev)
        self.nc.all_engine_barrier()
        for h in self.sem.values():
            self.nc.gpsimd.sem_clear(h)
        self.nc.all_engine_barrier()
        self.reset_state()


class Ring:
    def __init__(self, items):
        self.items = items
        self.i = 0

    def next(self):
        it = self.items[self.i % len(self.items)]
        self.i += 1
        return it


class Cfg:
    def __init__(self, nseq, L, nlayer=NLAYER, upto=99, debug=False):
        self.NSEQ = nseq
        self.L = L
        self.T = nseq * L
        self.NB = self.T // 512
        self.NT = self.T // 128
        self.NC = self.T // 64
        self.nlayer = nlayer
        self.upto = upto
        self.debug = debug


def bc(ap, shape):
    return ap.to_broadcast(list(shape))


def phase_p1(nc, em, cfg, l, dr, cst, xsrc):
    T, NB = cfg.T, cfg.NB
    NTS = cfg.L // 128
    with ExitStack() as st:
        def sb(name, shape, dt):
            return st.enter_context(nc.sbuf_tensor("%s_L%d" % (name, l), shape, dt))

        def pst(name, shape, dt):
            return st.enter_context(nc.psum_tensor("%s_L%d" % (name, l), shape, dt))

        W = sb("p1_w", [128, 8, DIN], BF16)
        tW = em.tiles("W", 8)
        wv = dr["w_in"][l].rearrange("(k p) n -> p k n", p=128)
        for k in range(8):
            em.dma("pool", lambda e, k=k: e.dma_start(out=W[:, k, :], in_=wv[:, k, :]),
                   writes=[tW[k]])

        xb = [sb("p1_xb%d" % i, [128, D], BF16) for i in range(8)]
        t_xb = em.tiles("xb", 8)
        rX = Ring(list(zip(xb, t_xb)))
        xq = []

        def load_x(n):
            xbi, txb = rX.next()
            em.dma("pool", lambda e: e.dma_start(out=xbi[:], in_=xsrc[n * 128:(n + 1) * 128, :]), writes=[txb])
            xq.append((xbi, txb))
        for n in range(4):
            load_x(n)
        xT = [sb("p1_xT%d" % i, [128, 8, 512], BF16) for i in range(2)]
        t_xT = [em.tiles("xT%d_" % i, 4) for i in range(2)]
        pT = [pst("p1_pT%d" % i, [128, 1024], BF16) for i in range(2)]
        t_pT = em.tiles("pT", 2)
        pA = [pst("p1_pA%d" % i, [128, 512], F32) for i in range(2)]
        t_pA = em.tiles("pA", 2)
        pF = [pst("p1_pF%d" % i, [128, 512], F32) for i in range(3)]
        t_pF = em.tiles("pF", 3)
        pQ = pst("p1_pQ", [128, 1024], BF16)
        t_pQ = em.tile("pQ")
        rT, rA, rF = Ring(list(zip(pT, t_pT))), Ring(list(zip(pA, t_pA))), Ring(list(zip(pF, t_pF)))

        qrot = [sb("p1_qrot%d" % i, [128, 8, 64], BF16) for i in range(2)]
        t_qrot = em.tiles("qrot", 2)
        krot = [sb("p1_krot%d" % i, [128, 2, 64], BF16) for i in range(2)]
        t_krot = em.tiles("krot", 2)
        vst = [sb("p1_v%d" % i, [128, 128], BF16) for i in range(2)]
        t_vst = em.tiles("vst", 2)
        rtmp = [sb("p1_rtmp%d" % i, [128, 4, 10, 8], F32) for i in range(2)]
        t_rtmp = [em.tiles("rtmp%d_" % i, 8) for i in range(2)]
        qts = [sb("p1_qts%d" % i, [64, 10, 128], BF16) for i in range(2)]
        t_qts = em.tiles("qts", 2)
        his = [sb("p1_his%d" % i, [128, 512], BF16) for i in range(2)]
        t_his = em.tiles("his", 2)
        hgs = [sb("p1_hgs%d" % i, [128, 512], BF16) for i in range(2)]
        t_hgs = em.tiles("hgs", 2)
        sq = sb("p1_sq", [128, 4, 512], F32)
        t_sq = em.tiles("sq", 4)
        NG = 2
        sig = [sb("p1_sig%d" % i, [128, 512], F32) for i in range(3)]
        t_sig = em.tiles("sig", 3)
        gg = [sb("p1_g%d" % i, [128, 512], F32) for i in range(NG)]
        t_g = em.tiles("g", NG)
        bb = [sb("p1_b%d" % i, [128, 512], F32) for i in range(NG)]
        t_b = em.tiles("b", NG)
        rel = [sb("p1_rel%d" % i, [128, 512], F32) for i in range(NG)]
        t_rel = em.tiles("rel", NG)
        eq = [sb("p1_eq%d" % i, [128, 512], F32) for i in range(NG)]
        t_eq = em.tiles("eq", NG)
        ek = [sb("p1_ek%d" % i, [128, 512], F32) for i in range(NG)]
        t_ek = em.tiles("ek", NG)
        tmr = [sb("p1_tmr%d" % i, [128, 8], F32) for i in range(NG)]
        t_tmr = em.tiles("tmr", NG)
        qk = [sb("p1_qk%d" % i, [128, 2, 4, 512], BF16) for i in range(2)]
        t_qk = [[[em.tile("qk") for _ in range(4)] for _ in range(2)] for _ in range(2)]
        scs = [sb("p1_scs%d" % i, [128, 6, 4, 8], F32) for i in range(2)]
        t_scs = [em.tiles("scs%d_" % i, 8) for i in range(2)]
        sg = [sb("p1_sg%d" % i, [128, 8, 512], BF16) for i in range(2)]
        t_sg = [em.tiles("sg%d_" % i, 8) for i in range(2)]

        ident = cst["ident"]
        lb_i = lambda d, p: cst["lbt"][:, (l * 2 + d) * 4 + p:(l * 2 + d) * 4 + p + 1]
        oml_i = lambda d, p: cst["oml"][:, (l * 2 + d) * 4 + p:(l * 2 + d) * 4 + p + 1]

        gidx = 0
        for b in range(NB):
            tau0 = b * 512
            xTi, txT = xT[b % 2], t_xT[b % 2]
            tqk = t_qk
            scsi, tscs = scs[b % 2], t_scs[b % 2]
            def emit_xT(nb):
                xTn, txTn = xT[nb % 2], t_xT[nb % 2]
                for tt in range(4):
                    p_, tp_ = rT.next()
                    xbi, txb = xq.pop(0)

                    def f(pe, p_=p_, xbi=xbi):
                        for k in range(8):
                            ins = pe.transpose(out=p_[:, k * 128:(k + 1) * 128],
                                               in_=xbi[:, k * 128:(k + 1) * 128], identity=ident[:])
                        return ins
                    em.op("pe", f, reads=[txb], writes=[tp_])
                    em.op("act", lambda e, p_=p_, tt=tt: e.activation(
                        out=xTn[:, :, tt * 128:(tt + 1) * 128],
                        in_=p_[:].rearrange("p (k j) -> p k j", j=128), func=AF.Copy),
                        reads=[tp_], writes=[txTn[tt]])
            if b == 0:
                emit_xT(0)
            for tt in range(4):
                ti = (tau0 // 128 + tt)
                tis = ti % NTS
                cosb = cst["cos"][:, tis, :]
                sinb = cst["sin"][:, tis, :]
                s2 = ti % 2
                p_, tp_ = rA.next()

                def fq(pe, p_=p_, tt=tt):
                    for k in range(8):
                        ins = pe.matmul(p_[:], lhsT=xTi[:, k, tt * 128:(tt + 1) * 128],
                                        rhs=W[:, k, C_AQ:C_AQ + 512], start=(k == 0), stop=(k == 7))
                    return ins
                em.op("pe", fq, reads=[txT[tt]] + tW, writes=[tp_])
                pv = p_[:].rearrange("p (h e) -> p h e", e=64)
                rt, trt = rtmp[s2], t_rtmp[s2]
                qr, tqr = qrot[s2], t_qrot[s2]
                cb8 = bc(cosb.unsqueeze(1), [128, 8, 8])
                sb8 = bc(sinb.unsqueeze(1), [128, 8, 8])
                em.op("dve", lambda e, pv=pv, rt=rt, cb8=cb8: e.tensor_tensor(
                    out=rt[:, 0, 0:8, :], in0=pv[:, :, 0:8], in1=cb8, op=ALU.mult), reads=[tp_], writes=[trt[0]])
                em.op("dve", lambda e, pv=pv, rt=rt, sb8=sb8: e.tensor_tensor(
                    out=rt[:, 1, 0:8, :], in0=pv[:, :, 8:16], in1=sb8, op=ALU.mult), reads=[tp_], writes=[trt[1]])
                em.op("dve", lambda e, pv=pv, rt=rt, cb8=cb8: e.tensor_tensor(
                    out=rt[:, 2, 0:8, :], in0=pv[:, :, 8:16], in1=cb8, op=ALU.mult), reads=[tp_], writes=[trt[2]])
                em.op("dve", lambda e, pv=pv, rt=rt, sb8=sb8: e.tensor_tensor(
                    out=rt[:, 3, 0:8, :], in0=pv[:, :, 0:8], in1=sb8, op=ALU.mult), reads=[tp_], writes=[trt[3]])
                em.op("pool", lambda e, rt=rt, qr=qr: e.tensor_tensor(
                    out=qr[:, :, 0:8], in0=rt[:, 0, 0:8, :], in1=rt[:, 1, 0:8, :], op=ALU.subtract),
                    reads=trt[0:2], writes=[tqr])
                em.op("pool", lambda e, rt=rt, qr=qr: e.tensor_tensor(
                    out=qr[:, :, 8:16], in0=rt[:, 2, 0:8, :], in1=rt[:, 3, 0:8, :], op=ALU.add),
                    reads=trt[2:4], writes=[tqr])
                em.op("act", lambda e, pv=pv, qr=qr: e.activation(
                    out=qr[:, :, 16:64], in_=pv[:, :, 16:64], func=AF.Copy), reads=[tp_], writes=[tqr])
                p2, tp2 = rA.next()

                def fkv(pe, p2=p2, tt=tt):
                    for k in range(8):
                        ins = pe.matmul(p2[:, 0:256], lhsT=xTi[:, k, tt * 128:(tt + 1) * 128],
                                        rhs=W[:, k, C_AK:C_AK + 256], start=(k == 0), stop=(k == 7))
                    return ins
                em.op("pe", fkv, reads=[txT[tt]] + tW, writes=[tp2])
                kv_ = p2[:, 0:128].rearrange("p (h e) -> p h e", e=64)
                kr, tkr = krot[s2], t_krot[s2]
                cb2 = bc(cosb.unsqueeze(1), [128, 2, 8])
                sb2 = bc(sinb.unsqueeze(1), [128, 2, 8])
                em.op("dve", lambda e, kv_=kv_, rt=rt, cb2=cb2: e.tensor_tensor(
                    out=rt[:, 0, 8:10, :], in0=kv_[:, :, 0:8], in1=cb2, op=ALU.mult), reads=[tp2], writes=[trt[4]])
                em.op("dve", lambda e, kv_=kv_, rt=rt, sb2=sb2: e.tensor_tensor(
                    out=rt[:, 1, 8:10, :], in0=kv_[:, :, 8:16], in1=sb2, op=ALU.mult), reads=[tp2], writes=[trt[5]])
                em.op("dve", lambda e, kv_=kv_, rt=rt, cb2=cb2: e.tensor_tensor(
                    out=rt[:, 2, 8:10, :], in0=kv_[:, :, 8:16], in1=cb2, op=ALU.mult), reads=[tp2], writes=[trt[6]])
                em.op("dve", lambda e, kv_=kv_, rt=rt, sb2=sb2: e.tensor_tensor(
                    out=rt[:, 3, 8:10, :], in0=kv_[:, :, 0:8], in1=sb2, op=ALU.mult), reads=[tp2], writes=[trt[7]])
                em.op("pool", lambda e, rt=rt, kr=kr: e.tensor_tensor(
                    out=kr[:, :, 0:8], in0=rt[:, 0, 8:10, :], in1=rt[:, 1, 8:10, :], op=ALU.subtract),
                    reads=trt[4:6], writes=[tkr])
                em.op("pool", lambda e, rt=rt, kr=kr: e.tensor_tensor(
                    out=kr[:, :, 8:16], in0=rt[:, 2, 8:10, :], in1=rt[:, 3, 8:10, :], op=ALU.add),
                    reads=trt[6:8], writes=[tkr])
                em.op("act", lambda e, kv_=kv_, kr=kr: e.activation(
                    out=kr[:, :, 16:64], in_=kv_[:, :, 16:64], func=AF.Copy), reads=[tp2], writes=[tkr])
                vs, tvs = vst[s2], t_vst[s2]
                em.op("act", lambda e, p2=p2, vs=vs: e.activation(
                    out=vs[:], in_=p2[:, 128:256], func=AF.Copy), reads=[tp2], writes=[tvs])
                em.dma("sp", lambda e, vs=vs, ti=ti: e.dma_start(
                    out=dr["V"][ti * 128:(ti + 1) * 128, :], in_=vs[:]), reads=[tvs])
                p3, tp3 = rA.next()

                def fhi(pe, p3=p3, tt=tt):
                    for k in range(8):
                        ins = pe.matmul(p3[:], lhsT=xTi[:, k, tt * 128:(tt + 1) * 128],
                                        rhs=W[:, k, C_HI:C_HI + 512], start=(k == 0), stop=(k == 7))
                    return ins
                em.op("pe", fhi, reads=[txT[tt]] + tW, writes=[tp3])
                hs, ths = his[s2], t_his[s2]
                em.op("act", lambda e, p3=p3, hs=hs: e.activation(out=hs[:], in_=p3[:], func=AF.Copy),
                      reads=[tp3], writes=[ths])
                em.dma("sp", lambda e, hs=hs, ti=ti: e.dma_start(
                    out=dr["HI"][ti * 128:(ti + 1) * 128, :], in_=hs[:]), reads=[ths])
                p4, tp4 = rA.next()

                def fhg(pe, p4=p4, tt=tt):
                    for k in range(8):
                        ins = pe.matmul(p4[:], lhsT=xTi[:, k, tt * 128:(tt + 1) * 128],
                                        rhs=W[:, k, C_HG:C_HG + 512], start=(k == 0), stop=(k == 7))
                    return ins
                em.op("pe", fhg, reads=[txT[tt]] + tW, writes=[tp4])
                gs, tgs = hgs[s2], t_hgs[s2]
                em.op("act", lambda e, p4=p4, gs=gs: e.activation(out=gs[:], in_=p4[:], func=AF.Silu),
                      reads=[tp4], writes=[tgs])
                em.dma("sp", lambda e, gs=gs, ti=ti: e.dma_start(
                    out=dr["SHG"][ti * 128:(ti + 1) * 128, :], in_=gs[:]), reads=[tgs])

                pk_, tpk_ = rT.next()

                def ftr(pe, qr=qr, kr=kr, pk_=pk_):
                    for h in range(8):
                        pe.transpose(out=pQ[0:64, h * 128:(h + 1) * 128], in_=qr[:, h, :], identity=ident[:])
                    for h in range(2):
                        ins = pe.transpose(out=pk_[0:64, h * 128:(h + 1) * 128], in_=kr[:, h, :], identity=ident[:])
                    return ins
                em.op("pe", ftr, reads=[tqr, tkr], writes=[t_pQ, tpk_])
                qt_, tqt = qts[s2], t_qts[s2]
                em.op("dve", lambda e, qt_=qt_: e.tensor_copy(
                    out=qt_[:, 0:8, :], in_=pQ[0:64, :].rearrange("p (h j) -> p h j", j=128)),
                    reads=[t_pQ], writes=[tqt])
                em.op("dve", lambda e, qt_=qt_, pk_=pk_: e.tensor_copy(
                    out=qt_[:, 8:10, :], in_=pk_[0:64, 0:256].rearrange("p (h j) -> p h j", j=128)),
                    reads=[tpk_], writes=[tqt])
                em.dma("sp", lambda e, qt_=qt_, ti=ti: e.dma_start(
                    out=dr["QT"][:, :, ti * 128:(ti + 1) * 128], in_=qt_[:, 0:8, :]), reads=[tqt])
                em.dma("sp", lambda e, qt_=qt_, ti=ti: e.dma_start(
                    out=dr["KT"][:, :, ti * 128:(ti + 1) * 128], in_=qt_[:, 8:10, :]), reads=[tqt])

            if b + 1 < NB:
                for n in range(4):
                    load_x((b + 1) * 4 + n)

            def fmm(col0):
                p_, tp_ = rF.next()

                def f(pe, p_=p_):
                    for k in range(8):
                        ins = pe.matmul(p_[:], lhsT=W[:, k, col0:col0 + 128], rhs=xTi[:, k, :],
                                        start=(k == 0), stop=(k == 7))
                    return ins
                em.op("pe", f, reads=txT + tW, writes=[tp_])
                return p_, tp_

            for p in range(4):
                p_, tp_ = fmm(C_HQ + p * 128)
                em.op("act", lambda e, p_=p_, p=p: e.activation(out=sq[:, p, :], in_=p_[:], func=AF.Silu),
                      reads=[tp_], writes=[t_sq[p]])
            gates = [(gsel, ft) for gsel in range(2) for ft in range(8)]

            def gate_group(gsel, ft):
                p_, tp_ = fmm((C_GA if gsel == 0 else C_GB) + ft * 128)
                em.op("act", lambda e: e.activation(
                    out=sg[gsel][:, ft, :], in_=p_[:], func=AF.Sigmoid), reads=[tp_], writes=[t_sg[gsel][ft]])
                if ft == 7:
                    dst = dr["SGA" if gsel == 0 else "SGB"]
                    em.dma("sp", lambda e: e.dma_start(
                        out=dst[:, tau0:tau0 + 512].rearrange("(f q) t -> q f t", q=128),
                        in_=sg[gsel][:]), reads=t_sg[gsel])

            def stage_a(n):
                d, p = divmod(n, 4)
                i3 = n % 3
                p_, tp_ = fmm((C_HFF if d == 0 else C_HFB) + p * 128)
                em.op("act", lambda e: e.activation(out=sig[i3][:], in_=p_[:], func=AF.Sigmoid),
                      reads=[tp_], writes=[t_sig[i3]])
                em.op("dve", lambda e: e.tensor_scalar(
                    out=sig[i3][:], in0=sig[i3][:], scalar1=oml_i(d, p), scalar2=lb_i(d, p),
                    op0=ALU.mult, op1=ALU.add), reads=[t_sig[i3]], writes=[t_sig[i3]])

            def stage_b(n):
                d, p = divmod(n, 4)
                i3, gi = n % 3, n % 2
                em.op("act", lambda e: e.activation(out=gg[gi][:], in_=sig[i3][:], func=AF.Ln),
                      reads=[t_sig[i3]], writes=[t_g[gi]])
                em.op("pool", lambda e: e.tensor_scalar(
                    out=sig[i3][:], in0=sig[i3][:], scalar1=-1.0, scalar2=1.0, op0=ALU.mult, op1=ALU.add),
                    reads=[t_sig[i3]], writes=[t_sig[i3]])
                em.op("dve", lambda e: e.tensor_tensor_scan(
                    out=bb[gi][:], data0=cst["cmask"][:], data1=gg[gi][:], initial=0.0,
                    op0=ALU.mult, op1=ALU.add), reads=[t_g[gi]], writes=[t_b[gi]])
                bv = bb[gi][:].rearrange("p (c j) -> p c j", j=64)
                rv = rel[gi][:].rearrange("p (c j) -> p c j", j=64)
                gv = gg[gi][:].rearrange("p (c j) -> p c j", j=64)
                if d == 0:
                    em.op("pool", lambda e: e.tensor_tensor(
                        out=rv, in0=bv, in1=bc(bv[:, :, 31:32], [128, 8, 64]), op=ALU.subtract),
                        reads=[t_b[gi]], writes=[t_rel[gi]])
                else:
                    em.op("pool", lambda e: e.tensor_tensor(
                        out=gg[gi][:], in0=bb[gi][:], in1=gg[gi][:], op=ALU.subtract),
                        reads=[t_b[gi], t_g[gi]], writes=[t_g[gi]])
                    em.op("pool", lambda e: e.tensor_tensor(
                        out=rv, in0=gv, in1=bc(gv[:, :, 32:33], [128, 8, 64]), op=ALU.subtract),
                        reads=[t_g[gi]], writes=[t_rel[gi]])
                    em.op("pool", lambda e: e.tensor_tensor(
                        out=tmr[gi][:], in0=bv[:, :, 63], in1=gv[:, :, 32], op=ALU.subtract),
                        reads=[t_b[gi], t_g[gi]], writes=[t_tmr[gi]])

            def stage_c(n):
                d, p = divmod(n, 4)
                i3, gi = n % 3, n % 2
                bv = bb[gi][:].rearrange("p (c j) -> p c j", j=64)
                rv = rel[gi][:].rearrange("p (c j) -> p c j", j=64)
                gv = gg[gi][:].rearrange("p (c j) -> p c j", j=64)
                sgn = 1.0 if d == 0 else -1.0
                em.op("act", lambda e: e.activation(
                    out=eq[gi][:], in_=rel[gi][:], func=AF.Exp, scale=sgn), reads=[t_rel[gi]], writes=[t_eq[gi]])
                em.op("act", lambda e: e.activation(
                    out=ek[gi][:], in_=rel[gi][:], func=AF.Exp, scale=-sgn), reads=[t_rel[gi]], writes=[t_ek[gi]])
                tsc = tscs[d * 4 + p]
                em.op("act", lambda e: e.activation(
                    out=scsi[:, 3 * d + 0, p, :], in_=bv[:, :, 63], func=AF.Exp), reads=[t_b[gi]], writes=[tsc])
                if d == 0:
                    em.op("act", lambda e: e.activation(
                        out=scsi[:, 1, p, :], in_=rv[:, :, 63], func=AF.Exp), reads=[t_rel[gi]], writes=[tsc])
                    em.op("act", lambda e: e.activation(
                        out=scsi[:, 2, p, :], in_=bv[:, :, 31], func=AF.Exp), reads=[t_b[gi]], writes=[tsc])
                else:
                    em.op("act", lambda e: e.activation(
                        out=scsi[:, 4, p, :], in_=gv[:, :, 32], func=AF.Exp), reads=[t_g[gi]], writes=[tsc])
                    em.op("act", lambda e: e.activation(
                        out=scsi[:, 5, p, :], in_=tmr[gi][:], func=AF.Exp), reads=[t_tmr[gi]], writes=[tsc])
                em.op("dve", lambda e: e.scalar_tensor_tensor(
                    out=qk[d][:, 0, p, :], in0=sq[:, p, :], scalar=0.125, in1=eq[gi][:],
                    op0=ALU.mult, op1=ALU.mult), reads=[t_sq[p], t_eq[gi]], writes=[tqk[d][0][p]])
                em.op("pool", lambda e: e.tensor_tensor(
                    out=qk[d][:, 1, p, :], in0=sig[i3][:], in1=ek[gi][:], op=ALU.mult),
                    reads=[t_sig[i3], t_ek[gi]], writes=[tqk[d][1][p]])
                if p == 3:
                    for w_, nm in ((0, "QM"), (1, "KM")):
                        dst = dr[nm + ("F" if d == 0 else "B")]
                        em.dma("sp", lambda e, dst=dst, w_=w_: e.dma_start(
                            out=dst[:, tau0:tau0 + 512].rearrange("(p q) t -> q p t", q=128),
                            in_=qk[d][:, w_, :, :]), reads=tqk[d][w_], key=tqk[d][w_][0])

            gq = list(gates)
            for it in range(10):
                if it < 8:
                    stage_a(it)
                    for _ in range(2):
                        gate_group(*gq.pop(0))
                if it == 8 and b + 1 < NB:
                    emit_xT(b + 1)
                if 1 <= it <= 8:
                    stage_b(it - 1)
                if it >= 2:
                    stage_c(it - 2)
            em.dma("sp", lambda e: e.dma_start(out=dr["SC"][b], in_=scsi[:]), reads=tscs)
        em.phase_end()


def phase_p2(nc, em, cfg, l, dr, cst):
    L, NSEQ = cfg.L, cfg.NSEQ
    NTS = L // 128
    with ExitStack() as st:
        def sb(name, shape, dt):
            return st.enter_context(nc.sbuf_tensor("%s_L%d" % (name, l), shape, dt))

        def pst(name, shape, dt):
            return st.enter_context(nc.psum_tensor("%s_L%d" % (name, l), shape, dt))
        KTs = [sb("p2_kt%d" % i, [64, 2, L], BF16) for i in range(2)]
        t_KT = em.tiles("p2kt", 2)
        Vs = [sb("p2_v%d" % i, [128, NTS, 128], BF16) for i in range(2)]
        t_V = em.tiles("p2v", 2)
        rQ = Ring([(sb("p2_q%d" % i, [64, 8, 128], BF16), em.tile("p2q")) for i in range(3)])
        rE = Ring([(sb("p2_e%d" % i, [128, 512], BF16), em.tile("p2e")) for i in range(8)])
        rS = Ring([(pst("p2_ps%d" % i, [128, 512], F32), em.tile("p2ps")) for i in range(3)])
        rO = Ring([(pst("p2_po%d" % i, [64, 512], F32), em.tile("p2po")) for i in range(2)])
        rD = Ring([(pst("p2_pd%d" % i, [64, 512], F32), em.tile("p2pd")) for i in range(2)])
        rDs = Ring([(sb("p2_ds%d" % i, [64, 512], F32), em.tile("p2ds")) for i in range(2)])
        rOa = Ring([(sb("p2_oa%d" % i, [64, 512], BF16), em.tile("p2oa")) for i in range(3)])
        ones = cst["ones"]
        def load_seq(s):
            kt, tkt = KTs[s % 2], t_KT[s % 2]
            vv, tv = Vs[s % 2], t_V[s % 2]
            em.dma("sp", lambda e: e.dma_start(out=kt[:], in_=dr["KT"][:, :, s * L:(s + 1) * L]), writes=[tkt])
            em.dma("sp", lambda e: e.dma_start(
                out=vv[:], in_=dr["V"][s * L:(s + 1) * L, :].rearrange("(i p) f -> p i f", p=128)), writes=[tv])

        def load_q(n):
            s, i = divmod(n, NTS)
            tau = s * L + i * 128
            qt, tq = rQ.next()
            em.dma("sp", lambda e: e.dma_start(out=qt[:], in_=dr["QT"][:, :, tau:tau + 128]), writes=[tq])
            return qt, tq

        def emit_s(s, i, g, qt, tq):
            kt, tkt = KTs[s % 2], t_KT[s % 2]
            js = [j for j in (i - 1, i, i + 1) if 0 <= j < NTS]
            es = []
            for j in js:
                ps_, tps = rS.next()
                em.op("pe", lambda pe: pe.matmul(
                    ps_[:], lhsT=kt[:, g, j * 128:(j + 1) * 128],
                    rhs=qt[:, 4 * g:4 * g + 4, :].rearrange("p h q -> p (h q)"), start=True, stop=True),
                    reads=[tkt, tq], writes=[tps])
                e_, te = rE.next()
                em.op("act", lambda e: e.activation(out=e_[:], in_=ps_[:], func=AF.Exp, scale=0.125),
                      reads=[tps], writes=[te])
                if j != i:
                    m = 0 if j < i else 1
                    ev = e_[:].rearrange("p (h q) -> p h q", q=128)
                    em.op("pool", lambda e: e.tensor_tensor(
                        out=ev, in0=ev, in1=bc(cst["band"][:, m, :].unsqueeze(1), [128, 4, 128]),
                        op=ALU.mult), reads=[te], writes=[te])
                es.append((j, e_, te))
            return es

        def emit_pv(s, i, g, es):
            vv, tv = Vs[s % 2], t_V[s % 2]
            tau = s * L + i * 128
            po, tpo = rO.next()
            pd, tpd = rD.next()

            def fo(pe):
                for n, (j, e_, te) in enumerate(es):
                    ins = pe.matmul(po[:], lhsT=vv[:, j, g * 64:(g + 1) * 64], rhs=e_[:],
                                    start=(n == 0), stop=(n == len(es) - 1))
                return ins
            em.op("pe", fo, reads=[tv] + [x[2] for x in es], writes=[tpo])

            def fd(pe):
                for n, (j, e_, te) in enumerate(es):
                    ins = pe.matmul(pd[:], lhsT=ones[:, 0:64], rhs=e_[:],
                                    start=(n == 0), stop=(n == len(es) - 1))
                return ins
            em.op("pe", fd, reads=[x[2] for x in es], writes=[tpd])
            ds, tds = rDs.next()
            esk = bc(cst["esink"][:, l * 8 + 4 * g:l * 8 + 4 * g + 4].unsqueeze(2), [64, 4, 128])
            em.op("dve", lambda e: e.tensor_tensor(
                out=ds[:].rearrange("p (h q) -> p h q", q=128),
                in0=pd[:].rearrange("p (h q) -> p h q", q=128), in1=esk, op=ALU.add),
                reads=[tpd], writes=[tds])
            em.op("act", lambda e: e.activation(out=ds[:], in_=ds[:], func=AF.Ln), reads=[tds], writes=[tds])
            em.op("act", lambda e: e.activation(out=ds[:], in_=ds[:], func=AF.Exp, scale=-1.0), reads=[tds], writes=[tds])
            oa, toa = rOa.next()
            em.op("dve", lambda e: e.tensor_tensor(out=oa[:], in0=po[:], in1=ds[:], op=ALU.mult),
                  reads=[tpo, tds], writes=[toa])
            em.dma("sp", lambda e: e.dma_start(
                out=dr["OATT"].rearrange("(h d) t -> d h t", d=64)[:, 4 * g:4 * g + 4, tau:tau + 128],
                in_=oa[:].rearrange("p (h q) -> p h q", q=128)), reads=[toa])

        ntile = NSEQ * NTS
        load_seq(0)
        qnext = load_q(0)
        prev = None
        for n in range(ntile):
            s, i = divmod(n, NTS)
            qt, tq = qnext
            if i == min(1, NTS - 1) and s + 1 < NSEQ:
                load_seq(s + 1)
            if n + 1 < ntile:
                qnext = load_q(n + 1)
            for g in range(2):
                es = emit_s(s, i, g, qt, tq)
                if prev is not None:
                    emit_pv(*prev)
                prev = (s, i, g, es)
        emit_pv(*prev)
        em.phase_end()


def phase_p3(nc, em, cfg, l, dr, cst):
    L, NSEQ = cfg.L, cfg.NSEQ
    NTS, NCS, NBS = L // 128, L // 64, L // 512
    with ExitStack() as st:
        def sb(name, shape, dt):
            return st.enter_context(nc.sbuf_tensor("%s_L%d" % (name, l), shape, dt))

        def pst(name, shape, dt):
            return st.enter_context(nc.psum_tensor("%s_L%d" % (name, l), shape, dt))
        ident = cst["ident"]
        Vh = sb("p3_vh", [128, NTS, 512], BF16)
        t_Vh = em.tile("p3vh")
        scs = sb("p3_scs", [128, NBS, 6, 4, 8], F32)
        t_scs = em.tile("p3scs")
        sc2 = sb("p3_sc2", [128, 6, 4, NCS], F32)
        t_sc2 = em.tile("p3sc2")
        rKm = Ring([(sb("p3_km%d" % i, [128, L], BF16), em.tile("p3km")) for i in range(2)])
        rKt = Ring([(sb("p3_kt%d" % i, [128, 4, 128], BF16), em.tile("p3kt")) for i in range(2)])
        Ubuf = sb("p3_u", [128, NCS, 64], F32)
        t_U = em.tile("p3u")
        Sraw = sb("p3_sraw", [128, NCS + 2, 64], F32)
        t_Sraw = em.tile("p3sraw")
        Sr = [[sb("p3_sr%d%d" % (d, p), [128, NCS, 64], BF16) for p in range(4)] for d in range(2)]
        t_Sr = [[em.tile("p3sr") for p in range(4)] for d in range(2)]
        rQK = Ring([(sb("p3_qk%d" % i, [128, 4, 4, 128], BF16), em.tile("p3qk")) for i in range(2)])
        rV64 = Ring([(sb("p3_v64%d" % i, [64, 2, 512], BF16), em.tile("p3v64")) for i in range(2)])
        rG64 = Ring([(sb("p3_g64%d" % i, [64, 2, 512], BF16), em.tile("p3g64")) for i in range(2)])
        rATm = Ring([(sb("p3_atm%d" % i, [64, 2, 8, 64], BF16), em.tile("p3atm")) for i in range(2)])
        rSq = Ring([(sb("p3_sq%d" % i, [64, 8, 64], F32), em.tile("p3sq")) for i in range(2)])
        rOn = Ring([(sb("p3_on%d" % i, [64, 8, 64], F32), em.tile("p3on")) for i in range(2)])
        rSs = Ring([(sb("p3_ss%d" % i, [64, 8], F32), em.tile("p3ss")) for i in range(2)])
        rOh = Ring([(sb("p3_oh%d" % i, [64, 512], BF16), em.tile("p3oh")) for i in range(2)])
        rOT = Ring([(sb("p3_ot%d" % i, [128, 4, 128], BF16), em.tile("p3ot")) for i in range(2)])
        pTk = pst("p3_ptk", [128, 1024], BF16)
        t_pTk = em.tile("p3ptk")
        pU = [pst("p3_pu%d" % i, [128, 4, 128], F32) for i in range(2)]
        t_pU = em.tiles("p3pu", 2)
        pATf = [pst("p3_pat%d" % i, [128, 512], F32) for i in range(2)]
        pAT = [x[0:64, :] for x in pATf]
        t_pAT = em.tiles("p3pat", 2)
        pOf = pst("p3_po", [128, 512], F32)
        pO = pOf[0:64, :]
        t_pO = em.tile("p3po")
        pObf = pst("p3_pob", [128, 512], F32)
        pOb = pObf[0:64, :]
        t_pOb = em.tile("p3pob")
        usets = [([pU[0], pU[1]], t_pU),
                 ([x[:].rearrange("q (a b) -> q a b", b=128) for x in pATf], t_pAT),
                 ([pOf[:].rearrange("q (a b) -> q a b", b=128), pObf[:].rearrange("q (a b) -> q a b", b=128)], [t_pO, t_pOb])]
        ucount = [0]
        rOs = Ring([(sb("p3_os%d" % i, [64, 8, 64], F32), em.tile("p3os")) for i in range(2)])

        def load_ab(s):
            t0 = s * L
            em.dma("sp", lambda e: e.dma_start(
                out=Vh[:], in_=dr["HI"][t0:t0 + L, :].rearrange("(i p) f -> p i f", p=128)), writes=[t_Vh])
            em.dma("sp", lambda e: e.dma_start(
                out=scs[:], in_=dr["SC"][s * NBS:(s + 1) * NBS].rearrange("b q i p c -> q b i p c")),
                writes=[t_scs])
            for idx in range(6):
                em.op("pool", lambda e, idx=idx: e.tensor_copy(
                    out=sc2[:, idx, :, :].rearrange("q p (b c) -> q p b c", c=8),
                    in_=scs[:, :, idx, :, :].rearrange("q b p c -> q p b c")), reads=[t_scs], writes=[t_sc2])
            em.op("pool", lambda e: e.memset(sc2[:, 0, :, 0:1], 0.0), writes=[t_sc2])
            em.op("pool", lambda e: e.memset(sc2[:, 3, :, NCS - 1:NCS], 0.0), writes=[t_sc2])

        def load_km(s, dp):
            d, p = divmod(dp, 4)
            kmsrc = dr["KMF" if d == 0 else "KMB"]
            km, tkm = rKm.next()
            em.dma("sp", lambda e: e.dma_start(
                out=km[:], in_=kmsrc[p * 128:(p + 1) * 128, s * L:(s + 1) * L]), writes=[tkm])
            return km, tkm

        def load_c(s, i):
            tau = s * L + i * 128
            qk, tqk = rQK.next()
            for a, nm in enumerate(("QMF", "KMF", "QMB", "KMB")):
                em.dma("sp", lambda e, a=a, nm=nm: e.dma_start(
                    out=qk[:, a, :, :], in_=dr[nm][:, tau:tau + 128].rearrange("(p q) t -> q p t", q=128)),
                    writes=[tqk])
            v64, tv64 = rV64.next()
            g64, tg64 = rG64.next()
            em.dma("sp", lambda e: e.dma_start(
                out=v64[:], in_=dr["HI"][tau:tau + 128, :].rearrange("(c s) f -> s c f", s=64)), writes=[tv64])
            em.dma("sp", lambda e: e.dma_start(
                out=g64[:], in_=dr["SHG"][tau:tau + 128, :].rearrange("(c s) f -> s c f", s=64)), writes=[tg64])
            return qk, tqk, v64, tv64, g64, tg64

        def emit_tr(km, grp):
            kt, tkt = rKt.next()

            def ftr(pe):
                for n in range(4):
                    i = grp * 4 + n
                    ins = pe.transpose(out=pTk[:, n * 128:(n + 1) * 128],
                                       in_=km[0][:, i * 128:(i + 1) * 128], identity=ident[:])
                return ins
            em.op("pe", ftr, reads=[km[1]], writes=[t_pTk])
            em.op("act", lambda e: e.activation(
                out=kt[:].rearrange("q n f -> q (n f)"), in_=pTk[:, 0:512], func=AF.Copy),
                reads=[t_pTk], writes=[tkt])
            return kt, tkt

        def emit_u(d, p, grp, kt, tkt):
            pU_, t_pU_ = usets[ucount[0] % 3]
            ucount[0] += 1

            def fu(pe):
                for n in range(4):
                    i = grp * 4 + n
                    for hh in range(2):
                        ins = pe.matmul(pU_[hh][:, n, :],
                                        lhsT=kt[hh * 64:(hh + 1) * 64, n, :],
                                        rhs=Vh[hh * 64:(hh + 1) * 64, i, p * 128:(p + 1) * 128],
                                        start=True, stop=True)
                return ins
            em.op("pe", fu, reads=[tkt, t_Vh], writes=t_pU_)
            c0 = grp * 8
            for hh in range(2):
                for h2 in range(2):
                    rows = slice(h2 * 64, (h2 + 1) * 64)
                    em.op("dve", lambda e, rows=rows, h2=h2, hh=hh: e.tensor_tensor(
                        out=Ubuf[rows, c0 + hh:c0 + 8:2, :], in0=pU_[hh][rows, :, h2 * 64:(h2 + 1) * 64],
                        in1=bc(sc2[rows, 3 * d + 1, p, c0 + hh:c0 + 8:2].unsqueeze(2), [64, 4, 64]),
                        op=ALU.mult), reads=[t_pU_[hh], t_sc2], writes=[t_U])

        def emit_scan(d, p):
            zslot = 1 if d == 0 else NCS
            em.op("dve", lambda e: e.memset(Sraw[:, zslot, :], 0.0), writes=[t_Sraw])
            if NCS > 1:
                for e_ in range(64):
                    if d == 0:
                        o_ap, a_ap, u_ap = (Sraw[:, 2:NCS + 1, e_], sc2[:, 0, p, 0:NCS - 1],
                                            Ubuf[:, 0:NCS - 1, e_])
                    else:
                        o_ap = Sraw[:, NCS - 1:0:-1, e_]
                        a_ap = sc2[:, 3, p, NCS - 1:0:-1]
                        u_ap = Ubuf[:, NCS - 1:0:-1, e_]
                    em.op("dve", lambda e, o_ap=o_ap, a_ap=a_ap, u_ap=u_ap: e.tensor_tensor_scan(
                        out=o_ap, data0=a_ap, data1=u_ap, initial=0.0, op0=ALU.mult, op1=ALU.add),
                        reads=[t_U, t_sc2], writes=[t_Sraw])
            em.op("pool", lambda e: e.tensor_tensor(
                out=Sr[d][p][:], in0=Sraw[:, 1:NCS + 1, :],
                in1=bc(sc2[:, 3 * d + 2, p, :].unsqueeze(2), [128, NCS, 64]), op=ALU.mult),
                reads=[t_Sraw, t_sc2], writes=[t_Sr[d][p]])

        NG4 = NTS // 4
        load_ab(0)
        kms = {0: load_km(0, 0)}
        for s in range(NSEQ):
            t0 = s * L
            items = [(dp, grp) for dp in range(8) for grp in range(NG4)]
            if 1 not in kms:
                kms[1] = load_km(s, 1)
            ktn = emit_tr(kms[0], 0)
            for n, (dp, grp) in enumerate(items):
                d, p = divmod(dp, 4)
                kt, tkt = ktn
                if n + 1 < len(items):
                    dpn, grpn = items[n + 1]
                    if grpn == 0 and dpn + 1 < 8:
                        kms[dpn + 1] = load_km(s, dpn + 1)
                    ktn = emit_tr(kms[dpn], grpn)
                emit_u(d, p, grp, kt, tkt)
                if grp == NG4 - 1:
                    emit_scan(d, p)
            kms = {}
            allSr = [t_Sr[d][p] for d in range(2) for p in range(4)]

            def emit_tail(oh, toh, oT, toT, tok, tau, cc):
                def ft(pe):
                    for f in range(4):
                        ins = pe.transpose(out=pTk[:, f * 64:(f + 1) * 64], in_=oh[:, f * 128:(f + 1) * 128],
                                           identity=ident[0:64, 0:64])
                    return ins
                em.op("pe", ft, reads=[toh], writes=[t_pTk])
                em.op("act", lambda e: e.activation(
                    out=oT[:, :, tok], in_=pTk[:, 0:256].rearrange("q (f t) -> q f t", t=64), func=AF.Copy),
                    reads=[t_pTk], writes=[toT])
                if cc == 1:
                    em.dma("sp", lambda e: e.dma_start(
                        out=dr["OHGT"][:, tau:tau + 128].rearrange("(f q) t -> q f t", q=128), in_=oT[:]), reads=[toT])
            pend = None
            patsets = [(pAT, t_pAT),
                       ([pU[0][0:64, :, :].rearrange("s a b -> s (a b)"), pU[1][0:64, :, :].rearrange("s a b -> s (a b)")], t_pU)]

            def emit_at(qk, tqk, tok, cidx):
                pa, tpa = patsets[cidx % 2]

                def fat(pe):
                    for d in range(2):
                        for p in range(4):
                            for h2 in range(2):
                                rows = slice(h2 * 64, (h2 + 1) * 64)
                                slot = d * 4 + p
                                ins = pe.matmul(pa[h2][:, slot * 64:(slot + 1) * 64],
                                                lhsT=qk[rows, 2 * d + 1, p, tok], rhs=qk[rows, 2 * d, p, tok],
                                                start=True, stop=True)
                    return ins
                em.op("pe", fat, reads=[tqk], writes=tpa)
                atm, tatm = rATm.next()
                for h2 in range(2):
                    em.op("dve", lambda e, h2=h2: e.tensor_tensor(
                        out=atm[:, :, h2:8:2, :],
                        in0=pa[h2][:, :].rearrange("s (d p t) -> s d p t", d=2, t=64),
                        in1=bc(cst["tri"][:, :, :].unsqueeze(2), [64, 2, 4, 64]), op=ALU.mult),
                        reads=[tpa[h2]], writes=[tatm])
                return atm, tatm
            atnext = None
            cnext = load_c(s, 0)
            if s + 1 < NSEQ:
                load_ab(s + 1)
                kms[0] = load_km(s + 1, 0)
            for i in range(NTS):
                tau = t0 + i * 128
                qk, tqk, v64, tv64, g64, tg64 = cnext
                if i + 1 < NTS:
                    cnext = load_c(s, i + 1)
                oT, toT = rOT.next()
                for cc in range(2):
                    c = i * 2 + cc
                    tok = slice(cc * 64, (cc + 1) * 64)
                    if atnext is None:
                        atnext = emit_at(qk, tqk, tok, c)
                    atm, tatm = atnext
                    if cc == 0:
                        atnext = emit_at(qk, tqk, slice(64, 128), c + 1)
                    elif i + 1 < NTS:
                        atnext = emit_at(cnext[0], cnext[1], slice(0, 64), c + 1)
                    else:
                        atnext = None

                    def fo(pe, atm=atm, qk=qk, v64=v64, cc=cc, c=c, tok=tok):
                        for p in range(4):
                            for h2 in range(2):
                                hh = p * 2 + h2
                                rows = slice(h2 * 64, (h2 + 1) * 64)
                                o_ap = pO[:, hh * 64:(hh + 1) * 64]
                                ob_ap = pOb[:, hh * 64:(hh + 1) * 64]
                                if h2 == 0:
                                    for d in range(2):
                                        pe.matmul(o_ap, lhsT=atm[:, d, hh, :], rhs=v64[:, cc, hh * 64:(hh + 1) * 64],
                                                  start=(d == 0), stop=False)
                                        ins = pe.matmul(o_ap, lhsT=qk[rows, 2 * d, p, tok], rhs=Sr[d][p][rows, c, :],
                                                        start=False, stop=(d == 1))
                                else:
                                    for d in range(2):
                                        pe.matmul(o_ap, lhsT=atm[:, d, hh, :], rhs=v64[:, cc, hh * 64:(hh + 1) * 64],
                                                  start=(d == 0), stop=(d == 1))
                                        ins = pe.matmul(ob_ap, lhsT=qk[rows, 2 * d, p, tok], rhs=Sr[d][p][rows, c, :],
                                                        start=(d == 0), stop=(d == 1))
                        return ins
                    em.op("pe", fo, reads=[tatm, tqk, tv64] + allSr, writes=[t_pO, t_pOb])
                    sq, tsq = rSq.next()
                    on, ton = rOn.next()
                    ss, tss = rSs.next()
                    oh, toh = rOh.next()
                    osb, tos = rOs.next()
                    em.op("act", lambda e, osb=osb: e.activation(
                        out=osb[:].rearrange("t h e -> t (h e)"), in_=pO[:], func=AF.Copy),
                        reads=[t_pO], writes=[tos])
                    em.op("dve", lambda e, osb=osb: e.tensor_tensor(
                        out=osb[:, 1:8:2, :], in0=pOb[:].rearrange("t (h e) -> t h e", e=64)[:, 1:8:2, :],
                        in1=osb[:, 1:8:2, :], op=ALU.add), reads=[t_pOb, tos], writes=[tos])
                    pov = osb[:]
                    em.op("act", lambda e, sq=sq, osb=osb: e.activation(
                        out=sq[:].rearrange("t h e -> t (h e)"), in_=osb[:].rearrange("t h e -> t (h e)"),
                        func=AF.Square), reads=[tos], writes=[tsq])
                    em.op("dve", lambda e, sq=sq, ss=ss: e.tensor_reduce(
                        out=ss[:], in_=sq[:], axis=AX.X, op=ALU.add), reads=[tsq], writes=[tss])
                    em.op("act", lambda e, ss=ss: e.activation(
                        out=ss[:], in_=ss[:], func=AF.Sqrt, scale=1.0 / 64, bias=cst["eps_rms"][0:64, :]),
                        reads=[tss], writes=[tss])
                    em.op("dve", lambda e, ss=ss: e.reciprocal(out=ss[:], in_=ss[:]), reads=[tss], writes=[tss])
                    em.op("dve", lambda e, on=on, ss=ss, pov=pov: e.tensor_tensor(
                        out=on[:], in0=pov, in1=bc(ss[:].unsqueeze(2), [64, 8, 64]), op=ALU.mult),
                        reads=[tos, tss], writes=[ton])
                    em.op("pool", lambda e, on=on: e.tensor_tensor(
                        out=on[:], in0=on[:], in1=bc(cst["normg"][:, l, :].unsqueeze(1), [64, 8, 64]),
                        op=ALU.mult), reads=[ton], writes=[ton])
                    em.op("pool", lambda e, on=on, oh=oh, g64=g64, cc=cc: e.tensor_tensor(
                        out=oh[:], in0=on[:].rearrange("t h e -> t (h e)"), in1=g64[:, cc, :], op=ALU.mult),
                        reads=[ton, tg64], writes=[toh])

                    if pend is not None:
                        emit_tail(*pend)
                    pend = (oh, toh, oT, toT, tok, tau, cc)
            if pend is not None:
                emit_tail(*pend)
                pend = None
        em.phase_end()


def layer_norm_tile(nc, em, cst, pre, tpre, xn, txn, stats, tstats, gam, bet, t_gb):
    for hh in range(2):
        em.op("dve", lambda e, hh=hh: e.bn_stats(out=stats[:, hh, :], in_=pre[:, hh * 512:(hh + 1) * 512]),
              reads=[tpre], writes=[tstats[hh]])
    em.op("dve", lambda e: e.bn_aggr(out=stats[:, 2, 0:2], in_=stats[:, 0:2, :].rearrange("p a b -> p (a b)")),
          reads=tstats[0:2], writes=[tstats[2]])
    em.op("act", lambda e: e.activation(out=stats[:, 2, 2:3], in_=stats[:, 2, 1:2], func=AF.Sqrt,
                                        bias=cst["eps_ln"][:], scale=1.0), reads=[tstats[2]], writes=[tstats[3]])
    em.op("dve", lambda e: e.reciprocal(out=stats[:, 2, 3:4], in_=stats[:, 2, 2:3]), reads=[tstats[3]], writes=[tstats[4]])
    em.op("dve", lambda e: e.scalar_tensor_tensor(out=xn[:], in0=pre[:], scalar=stats[:, 2, 0:1], in1=gam[:],
                                                  op0=ALU.subtract, op1=ALU.mult), reads=[tpre, tstats[2], t_gb], writes=[txn])
    em.op("dve", lambda e: e.scalar_tensor_tensor(out=xn[:], in0=xn[:], scalar=stats[:, 2, 3:4], in1=bet[:],
                                                  op0=ALU.mult, op1=ALU.add), reads=[txn, tstats[4], t_gb], writes=[txn])


def phase_p5a(nc, em, cfg, l, dr, cst, xsrc):
    NB = cfg.NB
    with ExitStack() as st:
        def sb(name, shape, dt):
            return st.enter_context(nc.sbuf_tensor("%s_L%d" % (name, l), shape, dt))

        def pst(name, shape, dt):
            return st.enter_context(nc.psum_tensor("%s_L%d" % (name, l), shape, dt))
        ident = cst["ident"]
        Wpa = sb("a_wpa", [128, 4, D], BF16)
        Wph = sb("a_wph", [128, 4, D], BF16)
        Wo = sb("a_wo", [128, 8, D], BF16)
        gam = sb("a_gam", [128, D], F32)
        bet = sb("a_bet", [128, D], F32)
        t_w = em.tile("a_w")
        t_gb = em.tile("a_gb")
        em.dma("pool", lambda e: e.dma_start(out=Wpa[:], in_=dr["w_proj_att"][l].rearrange("(k q) n -> q k n", q=128)), writes=[t_w])
        em.dma("pool", lambda e: e.dma_start(out=Wph[:], in_=dr["w_proj_hgrn"][l].rearrange("(k q) n -> q k n", q=128)), writes=[t_w])
        for k in range(8):
            em.dma("pool", lambda e, k=k: e.dma_start(out=Wo[:, k, :], in_=dr["w_out"][l][k * 128:(k + 1) * 128, :]), writes=[t_w])
        rOa = Ring([(sb("a_oa%d" % i, [128, 4, 512], BF16), em.tile("a_oa")) for i in range(2)])
        rOh = Ring([(sb("a_oh%d" % i, [128, 4, 512], BF16), em.tile("a_oh")) for i in range(2)])
        rGa = Ring([(sb("a_ga%d" % i, [128, 8, 512], BF16), em.tile("a_ga")) for i in range(2)])
        rGb = Ring([(sb("a_gb%d" % i, [128, 8, 512], BF16), em.tile("a_gb")) for i in range(2)])
        rX = Ring([(sb("a_x%d" % i, [128, D], F32), em.tile("a_x")) for i in range(3)])
        mixT = sb("a_mix", [128, 8, 512], BF16)
        t_mix = em.tiles("a_mix", 8)
        rM1 = Ring([(sb("a_m1%d" % i, [128, 512], F32), em.tile("a_m1")) for i in range(2)])
        rM2 = Ring([(sb("a_m2%d" % i, [128, 512], F32), em.tile("a_m2")) for i in range(2)])
        rPre = Ring([(sb("a_pre%d" % i, [128, D], F32), em.tile("a_pre")) for i in range(2)])
        rXn = Ring([(sb("a_xn%d" % i, [128, D], F32), em.tile("a_xn")) for i in range(2)])
        rSt = Ring([(sb("a_st%d" % i, [128, 3, 6], F32), em.tiles("a_st", 5)) for i in range(2)])
        rXb = Ring([(sb("a_xb%d" % i, [128, D], BF16), em.tile("a_xb")) for i in range(2)])
        rXT = Ring([(sb("a_xT%d" % i, [128, 8, 128], BF16), em.tile("a_xT")) for i in range(2)])
        rP = Ring([(pst("a_ps%d" % i, [128, 512], F32), em.tile("a_ps")) for i in range(6)])
        rPT = Ring([(pst("a_pt%d" % i, [128, 1024], BF16), em.tile("a_pt")) for i in range(2)])
        def load_blk(b):
            tau0 = b * 512
            oa, toa = rOa.next()
            oh, toh = rOh.next()
            ga, tga = rGa.next()
            gb, tgb = rGb.next()
            em.dma("sp", lambda e: e.dma_start(
                out=oa[:], in_=dr["OATT"][:, tau0:tau0 + 512].rearrange("(k q) t -> q k t", q=128)), writes=[toa])
            em.dma("sp", lambda e: e.dma_start(
                out=oh[:], in_=dr["OHGT"][:, tau0:tau0 + 512].rearrange("(k q) t -> q k t", q=128)), writes=[toh])
            em.dma("sp", lambda e: e.dma_start(
                out=ga[:], in_=dr["SGA"][:, tau0:tau0 + 512].rearrange("(k q) t -> q k t", q=128)), writes=[tga])
            em.dma("sp", lambda e: e.dma_start(
                out=gb[:], in_=dr["SGB"][:, tau0:tau0 + 512].rearrange("(k q) t -> q k t", q=128)), writes=[tgb])
            return oa, toa, oh, toh, ga, tga, gb, tgb

        def load_x(n):
            xt, txt = rX.next()
            em.dma("sp", lambda e: e.dma_start(out=xt[:], in_=xsrc[n * 128:(n + 1) * 128, :]), writes=[txt])
            return xt, txt

        def emit_tr(xb, txb, tau):
            pT, tpT = rPT.next()

            def ftr(pe):
                for k in range(8):
                    ins = pe.transpose(out=pT[:, k * 128:(k + 1) * 128], in_=xb[:, k * 128:(k + 1) * 128], identity=ident[:])
                return ins
            em.op("pe", ftr, reads=[txb], writes=[tpT])
            xT, txT = rXT.next()
            em.op("act", lambda e: e.activation(
                out=xT[:].rearrange("q k t -> q (k t)"), in_=pT[:], func=AF.Copy), reads=[tpT], writes=[txT])
            em.dma("sp", lambda e: e.dma_start(
                out=dr["X1T"][:, tau:tau + 128].rearrange("(k q) t -> q k t", q=128), in_=xT[:]), reads=[txT])

        pend = None
        bnext = load_blk(0)
        xnext = load_x(0)
        em.dma("sp", lambda e: e.dma_start(out=gam[:], in_=dr["ln1_g"][l].partition_broadcast(128)), writes=[t_gb])
        em.dma("sp", lambda e: e.dma_start(out=bet[:], in_=dr["ln1_b"][l].partition_broadcast(128)), writes=[t_gb])
        for b in range(NB):
            tau0 = b * 512
            oa, toa, oh, toh, ga, tga, gb, tgb = bnext
            if b + 1 < NB:
                bnext = load_blk(b + 1)
            for ft in range(8):
                cols = slice(ft * 128, (ft + 1) * 128)
                p1, tp1 = rP.next()

                def fa(pe, p1=p1, cols=cols):
                    for h in range(4):
                        ins = pe.matmul(p1[:], lhsT=Wpa[:, h, cols], rhs=oa[:, h, :], start=(h == 0), stop=(h == 3))
                    return ins
                em.op("pe", fa, reads=[t_w, toa], writes=[tp1])
                m1, tm1 = rM1.next()
                em.op("dve", lambda e, m1=m1, p1=p1, ft=ft: e.tensor_tensor(
                    out=m1[:], in0=p1[:], in1=ga[:, ft, :], op=ALU.mult), reads=[tp1, tga], writes=[tm1])
                p2, tp2 = rP.next()

                def fh(pe, p2=p2, cols=cols):
                    for k in range(4):
                        ins = pe.matmul(p2[:], lhsT=Wph[:, k, cols], rhs=oh[:, k, :], start=(k == 0), stop=(k == 3))
                    return ins
                em.op("pe", fh, reads=[t_w, toh], writes=[tp2])
                m2, tm2 = rM2.next()
                em.op("dve", lambda e, m2=m2, p2=p2, ft=ft: e.tensor_tensor(
                    out=m2[:], in0=p2[:], in1=gb[:, ft, :], op=ALU.mult), reads=[tp2, tgb], writes=[tm2])
                em.op("dve", lambda e, m1=m1, m2=m2, ft=ft: e.tensor_tensor(
                    out=mixT[:, ft, :], in0=m1[:], in1=m2[:], op=ALU.add), reads=[tm1, tm2], writes=[t_mix[ft]])
            for tt in range(4):
                tau = tau0 + tt * 128
                xt, txt = xnext
                if b * 4 + tt + 1 < NB * 4:
                    xnext = load_x(b * 4 + tt + 1)
                pre, tpre = rPre.next()
                for hh in range(2):
                    p3, tp3 = rP.next()

                    def fw(pe, p3=p3, tt=tt, hh=hh):
                        for k in range(8):
                            ins = pe.matmul(p3[:], lhsT=mixT[:, k, tt * 128:(tt + 1) * 128],
                                            rhs=Wo[:, k, hh * 512:(hh + 1) * 512], start=(k == 0), stop=(k == 7))
                        return ins
                    em.op("pe", fw, reads=[t_w] + t_mix, writes=[tp3])
                    em.op("dve", lambda e, pre=pre, xt=xt, p3=p3, hh=hh: e.scalar_tensor_tensor(
                        out=pre[:, hh * 512:(hh + 1) * 512], in0=xt[:, hh * 512:(hh + 1) * 512], scalar=ALPHA,
                        in1=p3[:], op0=ALU.mult, op1=ALU.add), reads=[txt, tp3], writes=[tpre])
                xn, txn = rXn.next()
                stt, tstt = rSt.next()
                layer_norm_tile(nc, em, cst, pre, tpre, xn, txn, stt, tstt, gam, bet, t_gb)
                em.dma("sp", lambda e, xn=xn, tau=tau: e.dma_start(out=dr["X1"][tau:tau + 128, :], in_=xn[:]), reads=[txn])
                xb, txb = rXb.next()
                em.op("act", lambda e, xb=xb, xn=xn: e.activation(out=xb[:], in_=xn[:], func=AF.Copy), reads=[txn], writes=[txb])
                if pend is not None:
                    emit_tr(*pend)
                pend = (xb, txb, tau)
        emit_tr(*pend)
        em.phase_end()


def phase_p5b(nc, em, cfg, l, dr, cst, dst):
    NB = cfg.NB
    with ExitStack() as st:
        def sb(name, shape, dt):
            return st.enter_context(nc.sbuf_tensor("%s_L%d" % (name, l), shape, dt))

        def pst(name, shape, dt):
            return st.enter_context(nc.psum_tensor("%s_L%d" % (name, l), shape, dt))
        W2 = sb("b_w2", [128, 32, D], BF16)
        t_w2 = [em.tile("b_w2")]
        for k in range(32):
            em.dma("pool", lambda e, k=k: e.dma_start(out=W2[:, k, :], in_=dr["w_ff2"][l][k * 128:(k + 1) * 128, :]),
                   writes=t_w2)
        gam = sb("b_gam", [128, D], F32)
        bet = sb("b_bet", [128, D], F32)
        t_gb = em.tile("b_gb")
        rW1 = Ring([(sb("b_w1%d" % i, [128, 8, 512], BF16), em.tile("b_w1")) for i in range(4)])
        rXT = Ring([(sb("b_xT%d" % i, [128, 8, 512], BF16), em.tile("b_xT")) for i in range(2)])
        rX = Ring([(sb("b_x%d" % i, [128, D], F32), em.tile("b_x")) for i in range(3)])
        hT = sb("b_hT", [128, 32, 512], BF16)
        t_hT = em.tiles("b_hT", 32)
        rR = Ring([(sb("b_r%d" % i, [128, 512], F32), em.tile("b_r")) for i in range(3)])
        rPre = Ring([(sb("b_pre%d" % i, [128, D], F32), em.tile("b_pre")) for i in range(2)])
        rXn = Ring([(sb("b_xn%d" % i, [128, D], F32), em.tile("b_xn")) for i in range(2)])
        rSt = Ring([(sb("b_st%d" % i, [128, 3, 6], F32), em.tiles("b_st", 5)) for i in range(2)])
        rP = Ring([(pst("b_ps%d" % i, [128, 512], F32), em.tile("b_ps")) for i in range(4)])
        rP2 = Ring([(pst("b_pq%d" % i, [128, 512], F32), em.tile("b_pq")) for i in range(4)])
        w1v = dr["W1B"][l].rearrange("(k q) n -> q k n", q=128)
        def load_xT(b):
            xT, txT = rXT.next()
            em.dma("sp", lambda e: e.dma_start(
                out=xT[:], in_=dr["X1T"][:, b * 512:(b + 1) * 512].rearrange("(k q) t -> q k t", q=128)), writes=[txT])
            return xT, txT

        def load_w1(n):
            j = n % 8
            w1, tw1 = rW1.next()
            em.dma("sp", lambda e: e.dma_start(out=w1[:], in_=w1v[:, :, j * 512:(j + 1) * 512]), writes=[tw1])
            return w1, tw1

        def load_x(n):
            xt, txt = rX.next()
            em.dma("sp", lambda e: e.dma_start(out=xt[:], in_=dr["X1"][n * 128:(n + 1) * 128, :]), writes=[txt])
            return xt, txt

        xTnext = load_xT(0)
        w1q = [load_w1(0), load_w1(1), load_w1(2)]
        xnext = load_x(0)
        em.dma("sp", lambda e: e.dma_start(out=gam[:], in_=dr["ln2_g"][l].partition_broadcast(128)), writes=[t_gb])
        em.dma("sp", lambda e: e.dma_start(out=bet[:], in_=dr["ln2_b"][l].partition_broadcast(128)), writes=[t_gb])
        for b in range(NB):
            tau0 = b * 512
            xT, txT = xTnext
            if b + 1 < NB:
                xTnext = load_xT(b + 1)
            for j in range(8):
                w1, tw1 = w1q.pop(0)
                if b * 8 + j + 3 < NB * 8:
                    w1q.append(load_w1(b * 8 + j + 3))
                for f4 in range(4):
                    ft = j * 4 + f4
                    p1, tp1 = rP.next()

                    def f1(pe, p1=p1, w1=w1, f4=f4):
                        for k in range(8):
                            ins = pe.matmul(p1[:], lhsT=w1[:, k, f4 * 128:(f4 + 1) * 128], rhs=xT[:, k, :],
                                            start=(k == 0), stop=(k == 7))
                        return ins
                    em.op("pe", f1, reads=[tw1, txT], writes=[tp1])
                    r_, tr_ = rR.next()
                    em.op("act", lambda e, r_=r_, p1=p1: e.activation(out=r_[:], in_=p1[:], func=AF.Relu),
                          reads=[tp1], writes=[tr_])
                    em.op("pool", lambda e, r_=r_, ft=ft: e.tensor_tensor(
                        out=hT[:, ft, :], in0=r_[:], in1=r_[:], op=ALU.mult), reads=[tr_], writes=[t_hT[ft]])
            for tt in range(4):
                tau = tau0 + tt * 128
                xt, txt = xnext
                if b * 4 + tt + 1 < NB * 4:
                    xnext = load_x(b * 4 + tt + 1)
                pre, tpre = rPre.next()
                for hh in range(2):
                    p2, tp2 = rP2.next()

                    def f2(pe, p2=p2, tt=tt, hh=hh):
                        for k in range(32):
                            ins = pe.matmul(p2[:], lhsT=hT[:, k, tt * 128:(tt + 1) * 128],
                                            rhs=W2[:, k, hh * 512:(hh + 1) * 512], start=(k == 0), stop=(k == 31))
                        return ins
                    em.op("pe", f2, reads=t_hT + t_w2, writes=[tp2])
                    em.op("dve", lambda e, pre=pre, xt=xt, p2=p2, hh=hh: e.scalar_tensor_tensor(
                        out=pre[:, hh * 512:(hh + 1) * 512], in0=xt[:, hh * 512:(hh + 1) * 512], scalar=ALPHA,
                        in1=p2[:], op0=ALU.mult, op1=ALU.add), reads=[txt, tp2], writes=[tpre])
                xn, txn = rXn.next()
                stt, tstt = rSt.next()
                layer_norm_tile(nc, em, cst, pre, tpre, xn, txn, stt, tstt, gam, bet, t_gb)
                em.dma("sp", lambda e, xn=xn, tau=tau: e.dma_start(out=dst[tau:tau + 128, :], in_=xn[:]), reads=[txn])
        em.phase_end()

def declare_io(nc, cfg):
    T = cfg.T
    dr = {}

    def inp(name, shape):
        dr[name] = nc.dram_tensor(name, list(shape), F32, kind="ExternalInput").ap()
    inp("x", [T, D])
    inp("w_in", [2, D, DIN])
    inp("att_sink", [2, 8])
    inp("hgrn_lb", [2, 2, 512])
    inp("hgrn_norm_g", [2, 64])
    inp("w_proj_att", [2, 512, D])
    inp("w_proj_hgrn", [2, 512, D])
    inp("w_out", [2, D, D])
    inp("ln1_g", [2, D])
    inp("ln1_b", [2, D])
    inp("w_ff1", [2, D, DFF])
    inp("w_ff2", [2, DFF, D])
    inp("ln2_g", [2, D])
    inp("ln2_b", [2, D])
    inp("c_ident", [128, 128])
    inp("c_cos", [cfg.L, 8])
    inp("c_sin", [cfg.L, 8])
    inp("c_cmask", [128, 512])
    inp("c_tri", [2, 64, 64])
    inp("c_band", [2, 128, 128])
    dr["y"] = nc.dram_tensor("y", [T, D], F32, kind="ExternalOutput").ap()
    kind = "ExternalOutput" if cfg.debug else "Internal"

    def scr(name, shape, dt):
        dr[name] = nc.dram_tensor(name, list(shape), dt, kind=kind).ap()
    scr("QT", [64, 8, T], BF16)
    scr("KT", [64, 2, T], BF16)
    scr("V", [T, 128], BF16)
    scr("HI", [T, 512], BF16)
    scr("SHG", [T, 512], BF16)
    for n in ("QMF", "KMF", "QMB", "KMB"):
        scr(n, [512, T], BF16)
    scr("SC", [cfg.NB, 128, 6, 4, 8], F32)
    scr("SGA", [D, T], BF16)
    scr("SGB", [D, T], BF16)
    scr("OATT", [512, T], BF16)
    scr("OHGT", [512, T], BF16)
    scr("X1", [T, D], F32)
    scr("X1T", [D, T], BF16)
    scr("XL", [T, D], F32)
    scr("W1B", [2, D, DFF], BF16)
    return dr


def setup_consts(nc, em, cfg, dr, st):
    def sb(name, shape, dt):
        return st.enter_context(nc.sbuf_tensor(name, shape, dt))
    cst = {}
    NTS = cfg.L // 128
    cst["ident"] = sb("k_ident", [128, 128], BF16)
    cst["cos"] = sb("k_cos", [128, NTS, 8], F32)
    cst["sin"] = sb("k_sin", [128, NTS, 8], F32)
    cst["cmask"] = sb("k_cmask", [128, 512], F32)
    cst["tri"] = sb("k_tri", [64, 2, 64], F32)
    cst["band"] = sb("k_band", [128, 2, 128], BF16)
    cst["ones"] = sb("k_ones", [128, 64], BF16)
    cst["lbt"] = sb("k_lbt", [128, 16], F32)
    cst["oml"] = sb("k_oml", [128, 16], F32)
    cst["esink"] = sb("k_esink", [64, 16], F32)
    cst["normg"] = sb("k_normg", [64, 2, 64], F32)
    cst["eps_ln"] = sb("k_epsln", [128, 1], F32)
    cst["eps_rms"] = sb("k_epsrms", [128, 1], F32)
    raw = sb("k_lbraw", [128, 16], F32)
    tmp = sb("k_lbtmp", [128, 4, 8], F32)
    t = em.tile("consts")
    traw = em.tile("lbraw")
    em.dma("pool", lambda e: e.dma_start(out=cst["ident"][:], in_=dr["c_ident"]), writes=[t])
    em.dma("sp", lambda e: e.dma_start(out=cst["cos"][:], in_=dr["c_cos"].rearrange("(i p) f -> p i f", p=128)), writes=[t])
    em.dma("sp", lambda e: e.dma_start(out=cst["sin"][:], in_=dr["c_sin"].rearrange("(i p) f -> p i f", p=128)), writes=[t])
    em.dma("sp", lambda e: e.dma_start(out=cst["cmask"][:], in_=dr["c_cmask"]), writes=[t])
    em.dma("sp", lambda e: e.dma_start(out=cst["tri"][:], in_=dr["c_tri"].rearrange("d s t -> s d t")), writes=[t])
    em.dma("pool", lambda e: e.dma_start(out=cst["band"][:], in_=dr["c_band"].rearrange("d s t -> s d t")), writes=[t])
    with nc.allow_non_contiguous_dma("tiny parameter gathers"):
        em.dma("sp", lambda e: e.dma_start(
            out=raw[:].rearrange("q (a p) -> q a p", p=4),
            in_=dr["hgrn_lb"].rearrange("l d (p q) -> q (l d) p", q=128)), writes=[traw])
    em.dma("sp", lambda e: e.dma_start(
        out=cst["esink"][:], in_=dr["att_sink"].rearrange("l h -> (l h)").partition_broadcast(64)), writes=[t])
    em.dma("sp", lambda e: e.dma_start(
        out=cst["normg"][:].rearrange("p l e -> p (l e)"),
        in_=dr["hgrn_norm_g"].rearrange("l e -> (l e)").partition_broadcast(64)), writes=[t])
    em.op("dve", lambda e: e.memset(cst["ones"][:], 1.0), writes=[t])
    em.op("dve", lambda e: e.memset(cst["eps_ln"][:], LN_EPS), writes=[t])
    em.op("dve", lambda e: e.memset(cst["eps_rms"][:], RMS_EPS), writes=[t])
    e0, e1, den, rec = tmp[:, 0, :], tmp[:, 1, :], tmp[:, 2, :], tmp[:, 3, :]
    em.op("act", lambda e: e.activation(out=tmp[:, 0:2, :].rearrange("q a p -> q (a p)"), in_=raw[:], func=AF.Exp),
          reads=[traw], writes=[t])
    em.op("dve", lambda e: e.tensor_tensor(out=den, in0=e0, in1=e1, op=ALU.add), reads=[t], writes=[t])
    em.op("dve", lambda e: e.reciprocal(out=rec, in_=den), reads=[t], writes=[t])
    em.op("dve", lambda e: e.tensor_tensor(out=e0, in0=e0, in1=rec, op=ALU.mult), reads=[t], writes=[t])
    em.op("dve", lambda e: e.tensor_tensor(out=e1, in0=e1, in1=rec, op=ALU.mult), reads=[t], writes=[t])
    em.op("dve", lambda e: e.tensor_tensor(out=cst["lbt"][:, 0:8], in0=e0, in1=e0, op=ALU.subtract), reads=[t], writes=[t])
    em.op("dve", lambda e: e.tensor_tensor(out=den, in0=e0, in1=e1, op=ALU.add), reads=[t], writes=[t])
    em.op("dve", lambda e: e.tensor_tensor(out=cst["lbt"][:, 8:16], in0=den, in1=e0, op=ALU.subtract), reads=[t], writes=[t])
    em.op("dve", lambda e: e.tensor_scalar(out=cst["oml"][:], in0=cst["lbt"][:], scalar1=-1.0, scalar2=1.0,
                                           op0=ALU.mult, op1=ALU.add), reads=[t], writes=[t])
    em.op("act", lambda e: e.activation(out=cst["esink"][:], in_=cst["esink"][:], func=AF.Exp), reads=[t], writes=[t])
    tw1 = em.tile("w1b")
    for l in range(cfg.nlayer):
        for k in range(8):
            em.dma("pool", lambda e, l=l, k=k: e.dma_start(
                out=dr["W1B"][l][k * 128:(k + 1) * 128, :], in_=dr["w_ff1"][l][k * 128:(k + 1) * 128, :]), writes=[tw1])
    em.phase_end()
    return cst


def build(cfg):
    nc = bass.Bass("TRN2", target_bir_lowering=False)
    dr = declare_io(nc, cfg)
    with ExitStack() as st:
        em = Em(nc, st)
        cst = setup_consts(nc, em, cfg, dr, st)
        for l in range(cfg.nlayer):
            last = (l == cfg.nlayer - 1)
            upto = cfg.upto if last else 99
            xsrc = dr["x"] if l == 0 else dr["XL"]
            if upto > 1:
                phase_p1(nc, em, cfg, l, dr, cst, xsrc)
            if upto > 2:
                phase_p2(nc, em, cfg, l, dr, cst)
            if upto > 3:
                phase_p3(nc, em, cfg, l, dr, cst)
            if upto > 4:
                phase_p5a(nc, em, cfg, l, dr, cst, xsrc)
            if upto > 5:
                phase_p5b(nc, em, cfg, l, dr, cst, dr["y"] if last else dr["XL"])
    return nc


def host_consts(L):
    inv = (ROPE_THETA ** (-np.arange(0, 16, 2, dtype=np.float32) / 16)).astype(np.float32)
    ang = np.arange(L, dtype=np.float32)[:, None] * inv[None, :]
    cmask = np.ones((128, 512), np.float32)
    cmask[:, ::64] = 0.0
    s = np.arange(64)[:, None]
    t = np.arange(64)[None, :]
    tri = np.stack([(t >= s), (t <= s)]).astype(np.float32)
    j = np.arange(128)[:, None]
    q = np.arange(128)[None, :]
    band = np.stack([(j >= q), (j <= q)]).astype(np.float32)
    return {
        "c_ident": np.eye(128, dtype=np.float32),
        "c_cos": np.cos(ang).astype(np.float32),
        "c_sin": np.sin(ang).astype(np.float32),
        "c_cmask": cmask, "c_tri": tri, "c_band": band,
    }


WNAMES = ("w_in", "att_sink", "hgrn_lb", "hgrn_norm_g", "w_proj_att", "w_proj_hgrn", "w_out",
          "ln1_g", "ln1_b", "w_ff1", "w_ff2", "ln2_g", "ln2_b")


def kernel(**inputs):
    xp = np.asarray(inputs["x_prompt"], np.float32)
    xs = np.asarray(inputs["x_sample"], np.float32)
    L = xp.shape[1]
    allx = np.concatenate([xp, xs], axis=0)
    ncore = 8
    nseq = allx.shape[0] // ncore
    cfg = Cfg(nseq, L)
    nc = build(cfg)
    shared = {k: np.ascontiguousarray(np.asarray(inputs[k], np.float32)) for k in WNAMES}
    shared.update(host_consts(L))
    in_maps = []
    for c in range(ncore):
        m = dict(shared)
        m["x"] = np.ascontiguousarray(allx[c * nseq:(c + 1) * nseq].reshape(nseq * L, D))
        in_maps.append(m)
    res = run_bass_kernel_spmd(nc, in_maps, core_ids=list(range(ncore)))
    ys = np.stack([np.asarray(r["y"], np.float32).reshape(nseq, L, D) for r in res.results], 0)
    ys = ys.reshape(ncore * nseq, L, D)
    nb = xp.shape[0]
    return (np.ascontiguousarray(ys[:nb]), np.ascontiguousarray(ys[nb:]))
```
